# Optimizing a Trainium2 kernel written in Bass

```python
import jax, jax.numpy as jnp
from jax import lax
import numpy as np

D_MODEL = 1024
BATCH = 4
SEQ = 8192
DEPTH = 1

GDN_HEADS = 4
GDN_DK = 128
GDN_DV = 128
CONV_K = 4
GLA_HEADS = 4
GLA_DK = 64
GLA_DV = 128
GLA_RANK = 16
GLA_NORMALIZER = 16.0
CHUNK = 64
D_FF = 2816
ALPHA = (2.0 * DEPTH) ** 0.25
INIT_BETA = (8.0 * DEPTH) ** -0.25
LN_EPS = 1e-5
RMS_EPS = 1e-6

GDN_QK_W = GDN_HEADS * GDN_DK
GDN_V_W = GDN_HEADS * GDN_DV
GLA_QK_W = GLA_HEADS * GLA_DK
GLA_V_W = GLA_HEADS * GLA_DV
MIX_W = GDN_V_W + GLA_V_W
CONV_CH = 2 * GDN_QK_W + GDN_V_W
PROJ_SPLITS = (CONV_CH, GDN_V_W, GDN_HEADS, GDN_HEADS, GLA_QK_W, GLA_QK_W, GLA_V_W, GLA_V_W, GLA_RANK)
PROJ_DIM = sum(PROJ_SPLITS)

kernel_name = "hybrid_gdn_gla_macaron_deepnorm"


def _split_points(sizes):
    pts, acc = [], 0
    for s in sizes[:-1]:
        acc += s
        pts.append(acc)
    return pts


def _layer_norm(x, g, b):
    xf = x.astype(jnp.float32)
    mu = jnp.mean(xf, -1, keepdims=True)
    var = jnp.mean(jnp.square(xf - mu), -1, keepdims=True)
    return ((xf - mu) * lax.rsqrt(var + LN_EPS) * g + b).astype(x.dtype)


def _swiglu(x, w_gate, w_up, w_down):
    return (jax.nn.silu(x @ w_gate) * (x @ w_up)) @ w_down


def _causal_conv(x, w):
    return lax.conv_general_dilated(
        x, w[:, None, :].astype(x.dtype), window_strides=(1,), padding=((CONV_K - 1, 0),),
        dimension_numbers=("NWC", "WIO", "NWC"), feature_group_count=x.shape[-1])


def _l2norm(x):
    return x * lax.rsqrt(jnp.sum(jnp.square(x), -1, keepdims=True) + RMS_EPS)


def _gated_rmsnorm(o, gate, w):
    o = o * lax.rsqrt(jnp.mean(jnp.square(o), -1, keepdims=True) + RMS_EPS) * w
    return o * jax.nn.silu(gate)


def _to_chunks(t):
    b, l, h, d = t.shape
    return t.reshape(b, l // CHUNK, CHUNK, h, d).transpose(0, 3, 1, 2, 4)


def _scalars_to_chunks(t):
    b, l, h = t.shape
    return t.reshape(b, l // CHUNK, CHUNK, h).transpose(0, 3, 1, 2)


def _from_scan(o):
    n, b, h, c, d = o.shape
    return o.transpose(1, 0, 3, 2, 4).reshape(b, n * c, h, d)


def _gated_delta_rule(q, k, v, beta, g):
    dv = v.shape[-1]
    q = q * (GDN_DK ** -0.5)
    G = jnp.cumsum(g, axis=-1)
    causal = jnp.tril(jnp.ones((CHUNK, CHUNK), bool))
    strict = jnp.tril(jnp.ones((CHUNK, CHUNK), bool), -1)
    decay = jnp.exp(jnp.where(causal, G[..., :, None] - G[..., None, :], -jnp.inf))
    kk = jnp.einsum("bhncd,bhnsd->bhncs", k, k)
    a_low = jnp.where(strict, beta[..., None] * kk * decay, 0.0)
    eye = jnp.eye(CHUNK, dtype=q.dtype)
    rhs = jnp.concatenate([v * beta[..., None], k * (beta * jnp.exp(G))[..., None]], -1)
    sol = lax.linalg.triangular_solve(a_low + eye, rhs, left_side=True, lower=True,
                                      unit_diagonal=True)
    u, w = sol[..., :dv], sol[..., dv:]
    qk = jnp.einsum("bhncd,bhnsd->bhncs", q, k) * decay
    q_dec = q * jnp.exp(G)[..., None]
    k_dec = k * jnp.exp(G[..., -1:] - G)[..., None]
    g_last = jnp.exp(G[..., -1])

    def step(S, xs):
        u_c, w_c, qk_c, qd_c, kd_c, gl_c = xs
        v_new = u_c - jnp.einsum("bhcd,bhde->bhce", w_c, S)
        o = jnp.einsum("bhcd,bhde->bhce", qd_c, S) + jnp.einsum("bhcs,bhse->bhce", qk_c, v_new)
        S = S * gl_c[..., None, None] + jnp.einsum("bhcd,bhce->bhde", kd_c, v_new)
        return S, o

    xs = tuple(jnp.moveaxis(t, 2, 0) for t in (u, w, qk, q_dec, k_dec, g_last))
    b, h = q.shape[0], q.shape[1]
    S0 = jnp.zeros((b, h, GDN_DK, dv), q.dtype)
    _, o = lax.scan(step, S0, xs)
    return _from_scan(o)


def _gla(q, k, v, logg):
    dv = v.shape[-1]
    q = q * (GLA_DK ** -0.5)
    bcum = jnp.cumsum(logg, axis=-2)
    causal = jnp.tril(jnp.ones((CHUNK, CHUNK), bool))[..., None]

    def step(S, xs):
        q_c, k_c, v_c, b_c = xs
        diff = b_c[:, :, :, None, :] - b_c[:, :, None, :, :]
        dec = jnp.exp(jnp.where(causal, diff, -jnp.inf))
        att = jnp.einsum("bhik,bhjk,bhijk->bhij", q_c, k_c, dec)
        o = jnp.einsum("bhik,bhkv->bhiv", q_c * jnp.exp(b_c), S) + jnp.einsum("bhij,bhjv->bhiv", att, v_c)
        b_last = b_c[:, :, -1]
        S = S * jnp.exp(b_last)[..., None] + jnp.einsum(
            "bhjk,bhjv->bhkv", k_c * jnp.exp(b_last[:, :, None] - b_c), v_c)
        return S, o

    xs = tuple(jnp.moveaxis(t, 2, 0) for t in (q, k, v, bcum))
    b, h = q.shape[0], q.shape[1]
    S0 = jnp.zeros((b, h, GLA_DK, dv), q.dtype)
    _, o = lax.scan(step, S0, xs)
    return _from_scan(o)


def _token_mixer(h, w_in, conv_w, a_log, dt_bias, gdn_norm_w, w_gk, b_gk, gla_norm_w, w_out):
    bsz, l, _ = h.shape
    f32 = jnp.float32
    proj = h @ w_in
    qkv_a, z_a, beta_a, dec_a, q_b, k_b, v_b, g_b, lr_b = jnp.split(proj, _split_points(PROJ_SPLITS), -1)

    qkv = jax.nn.silu(_causal_conv(qkv_a, conv_w)).astype(f32)
    q_a, k_a, v_a = jnp.split(qkv, [GDN_QK_W, 2 * GDN_QK_W], -1)
    q_a = _l2norm(q_a.reshape(bsz, l, GDN_HEADS, GDN_DK))
    k_a = _l2norm(k_a.reshape(bsz, l, GDN_HEADS, GDN_DK))
    v_a = v_a.reshape(bsz, l, GDN_HEADS, GDN_DV)
    beta = jax.nn.sigmoid(beta_a.astype(f32))
    g = -jnp.exp(a_log.astype(f32)) * jax.nn.softplus(dec_a.astype(f32) + dt_bias.astype(f32))
    o_a = _gated_delta_rule(_to_chunks(q_a), _to_chunks(k_a), _to_chunks(v_a),
                            _scalars_to_chunks(beta), _scalars_to_chunks(g))
    o_a = _gated_rmsnorm(o_a, z_a.astype(f32).reshape(bsz, l, GDN_HEADS, GDN_DV),
                         gdn_norm_w.astype(f32)).reshape(bsz, l, GDN_V_W)

    logg = jax.nn.log_sigmoid((lr_b @ w_gk + b_gk).astype(f32)) / GLA_NORMALIZER
    q_b = q_b.astype(f32).reshape(bsz, l, GLA_HEADS, GLA_DK)
    k_b = k_b.astype(f32).reshape(bsz, l, GLA_HEADS, GLA_DK)
    v_b = v_b.astype(f32).reshape(bsz, l, GLA_HEADS, GLA_DV)
    logg = logg.reshape(bsz, l, GLA_HEADS, GLA_DK)
    o_b = _gla(_to_chunks(q_b), _to_chunks(k_b), _to_chunks(v_b), _to_chunks(logg))
    o_b = _gated_rmsnorm(o_b, g_b.astype(f32).reshape(bsz, l, GLA_HEADS, GLA_DV),
                         gla_norm_w.astype(f32)).reshape(bsz, l, GLA_V_W)

    mix = jnp.concatenate([o_a, o_b], -1).astype(h.dtype)
    return mix @ w_out


def setup_inputs(seed: int = 0) -> dict:
    key = jax.random.key(seed)
    ks = jax.random.split(key, 24)
    nrm = lambda k, shape, s: jax.random.normal(k, shape, jnp.float32) * s
    x = jax.random.normal(ks[0], (BATCH, SEQ, D_MODEL), jnp.float32)

    col_scale = jnp.concatenate([
        jnp.ones((2 * GDN_QK_W,)), jnp.full((GDN_V_W,), INIT_BETA), jnp.ones((GDN_V_W + 2 * GDN_HEADS,)),
        jnp.ones((2 * GLA_QK_W,)), jnp.full((GLA_V_W,), INIT_BETA), jnp.ones((GLA_V_W + GLA_RANK,))])
    w_in = nrm(ks[1], (DEPTH, D_MODEL, PROJ_DIM), D_MODEL ** -0.5) * col_scale
    conv_w = nrm(ks[2], (DEPTH, CONV_K, CONV_CH), CONV_K ** -0.5)
    a_log = jnp.log(jax.random.uniform(ks[3], (DEPTH, GDN_HEADS), jnp.float32, 1.0, 16.0))
    dt = jnp.exp(jax.random.uniform(ks[4], (DEPTH, GDN_HEADS), jnp.float32, np.log(1e-3), np.log(1e-1)))
    dt_bias = dt + jnp.log(-jnp.expm1(-dt))
    gdn_norm_w = 1.0 + nrm(ks[5], (DEPTH, GDN_DV), 0.02)
    w_gk = nrm(ks[6], (DEPTH, GLA_RANK, GLA_QK_W), GLA_RANK ** -0.5)
    b_gk = nrm(ks[7], (DEPTH, GLA_QK_W), 0.1)
    gla_norm_w = 1.0 + nrm(ks[8], (DEPTH, GLA_DV), 0.02)
    w_out = nrm(ks[9], (DEPTH, MIX_W, D_MODEL), MIX_W ** -0.5 * INIT_BETA)

    def ffn(k):
        k1, k2, k3 = jax.random.split(k, 3)
        return (nrm(k1, (DEPTH, D_MODEL, D_FF), D_MODEL ** -0.5),
                nrm(k2, (DEPTH, D_MODEL, D_FF), D_MODEL ** -0.5 * INIT_BETA),
                nrm(k3, (DEPTH, D_FF, D_MODEL), D_FF ** -0.5 * INIT_BETA))

    f1g, f1u, f1d = ffn(ks[10])
    f2g, f2u, f2d = ffn(ks[11])
    ln = lambda kg, kb: (1.0 + nrm(kg, (DEPTH, D_MODEL), 0.02), nrm(kb, (DEPTH, D_MODEL), 0.02))
    ln1_g, ln1_b = ln(ks[12], ks[13])
    ln2_g, ln2_b = ln(ks[14], ks[15])
    ln3_g, ln3_b = ln(ks[16], ks[17])
    return {"x": x,
            "ffn1_w_gate": f1g, "ffn1_w_up": f1u, "ffn1_w_down": f1d, "ln1_g": ln1_g, "ln1_b": ln1_b,
            "w_in": w_in, "conv_w": conv_w, "a_log": a_log, "dt_bias": dt_bias, "gdn_norm_w": gdn_norm_w,
            "w_gk": w_gk, "b_gk": b_gk, "gla_norm_w": gla_norm_w, "w_out": w_out,
            "ln2_g": ln2_g, "ln2_b": ln2_b,
            "ffn2_w_gate": f2g, "ffn2_w_up": f2u, "ffn2_w_down": f2d, "ln3_g": ln3_g, "ln3_b": ln3_b}


def reference(x, ffn1_w_gate, ffn1_w_up, ffn1_w_down, ln1_g, ln1_b,
              w_in, conv_w, a_log, dt_bias, gdn_norm_w, w_gk, b_gk, gla_norm_w, w_out,
              ln2_g, ln2_b, ffn2_w_gate, ffn2_w_up, ffn2_w_down, ln3_g, ln3_b):
    for i in range(DEPTH):
        x = _layer_norm(ALPHA * x + 0.5 * _swiglu(x, ffn1_w_gate[i], ffn1_w_up[i], ffn1_w_down[i]),
                        ln1_g[i], ln1_b[i])
        mix = _token_mixer(x, w_in[i], conv_w[i], a_log[i], dt_bias[i], gdn_norm_w[i],
                           w_gk[i], b_gk[i], gla_norm_w[i], w_out[i])
        x = _layer_norm(ALPHA * x + mix, ln2_g[i], ln2_b[i])
        x = _layer_norm(ALPHA * x + 0.5 * _swiglu(x, ffn2_w_gate[i], ffn2_w_up[i], ffn2_w_down[i]),
                        ln3_g[i], ln3_b[i])
    return x
```

```python
import numpy as np
from contextlib import ExitStack
import concourse.bass as bass
import concourse.mybir as mybir
from concourse.bass_utils import run_bass_kernel_spmd

F32 = mybir.dt.float32
BF16 = mybir.dt.bfloat16
ALU = mybir.AluOpType
AF = mybir.ActivationFunctionType

D = 1024
KD = 8
FF = 2816
KF = 22
ALPHA = 2.0 ** 0.25
LN_EPS = 1e-5
RMS_EPS = 1e-6
NCORES = 8


class Sem:
    def __init__(self, h, name):
        self.h = h
        self.v = 0
        self.name = name


class Buf:
    __slots__ = ("name", "w", "r", "excl")

    def __init__(self, name, excl=False):
        self.name = name
        self.w = None
        self.r = {}
        self.excl = excl


class Eng:
    def __init__(self, name, h, is_pe=False):
        self.name = name
        self.h = h
        self.is_pe = is_pe
        self.sem = None
        self.waited = {}


class K:
    def __init__(self, nc):
        self.nc = nc
        self.pe = Eng("pe", nc.tensor, True)
        self.act = Eng("act", nc.scalar)
        self.dve = Eng("dve", nc.vector)
        self.pool = Eng("pool", nc.gpsimd)
        self.sp = Eng("sp", nc.sync)
        self.compute = [self.pe, self.act, self.dve, self.pool]
        self.all = self.compute + [self.sp]
        self.sems = []
        self.nsem = 0
        self.es = None
        self.ses = ExitStack()

    def new_sem(self, name):
        self.nsem += 1
        s = Sem(self.ses.enter_context(self.nc.semaphore(f"{name}_{self.nsem}")), name)
        self.sems.append(s)
        return s

    def begin_phase(self, es, tag):
        self.es = es
        for e in self.compute:
            e.sem = self.new_sem(f"{tag}_{e.name}")

    def _waits(self, eng, R, W):
        need = {}

        def add(tok):
            if tok is None:
                return
            s, v = tok
            if eng.is_pe and s is eng.sem:
                return
            if eng.waited.get(s, 0) >= v:
                return
            if need.get(s, 0) < v:
                need[s] = v

        for b in R:
            add(b.w)
            if b.excl:
                for s, v in b.r.items():
                    if s is not eng.sem:
                        add((s, v))
        for b in W:
            add(b.w)
            for s, v in b.r.items():
                add((s, v))
        for s, v in need.items():
            eng.h.wait_ge(s.h, v)
            eng.waited[s] = v

    def _record(self, tok, R, W):
        s, v = tok
        for b in R:
            if b.r.get(s, 0) < v:
                b.r[s] = v
        for b in W:
            b.w = tok
            b.r = {}

    def op(self, eng, fn, R=(), W=(), sig=True):
        self._waits(eng, R, W)
        ins = fn(eng.h)
        if sig:
            ins.then_inc(eng.sem.h, 1)
            eng.sem.v += 1
            tok = (eng.sem, eng.sem.v)
        else:
            tok = (eng.sem, eng.sem.v + 1)
        self._record(tok, R, W)
        return tok

    def dma(self, eng, pairs, sem, R=(), W=()):
        self._waits(eng, R, W)
        for o, i in pairs:
            eng.h.dma_start(out=o, in_=i).then_inc(sem.h, 16)
            sem.v += 16
        tok = (sem, sem.v)
        self._record(tok, R, W)
        return tok

    def barrier(self):
        for e in self.all:
            for s in self.sems:
                if s.v > 0 and e.waited.get(s, 0) < s.v and not (e.is_pe and s is e.sem):
                    e.h.wait_ge(s.h, s.v)
                    e.waited[s] = s.v


def make_ident(k, es, dt, name):
    nc = k.nc
    t = es.enter_context(nc.sbuf_tensor(name + "_f", [128, 128], F32))
    b = Buf(name)
    k.op(k.pool, lambda e: e.memset(t[:], 0.0), W=[b])
    k.op(k.pool, lambda e: e.affine_select(out=t[:], in_=t[:], pattern=[[-1, 128]], compare_op=ALU.not_equal,
                                           fill=1.0, base=0, channel_multiplier=1), R=[b], W=[b])
    if dt == F32:
        return t, b
    t2 = es.enter_context(nc.sbuf_tensor(name, [128, 128], dt))
    b2 = Buf(name + "c")
    k.op(k.pool, lambda e: e.tensor_copy(t2[:], t[:]), R=[b], W=[b2])
    return t2, b2


def phase_ffn(k, tag, src, dst, srcb, dstb, wg, wu, wd, lng, lnb, T):
    nc = k.nc
    G = 256
    NG = T // G
    with ExitStack() as es:
        k.begin_phase(es, tag)

        def sb(name, shape, dt):
            return es.enter_context(nc.sbuf_tensor(f"{tag}_{name}", shape, dt))

        def ps(name, shape, dt):
            return es.enter_context(nc.psum_tensor(f"{tag}_{name}", shape, dt))

        Wg = sb("Wg", [128, KD, FF], BF16)
        Wu = sb("Wu", [128, KD, FF], BF16)
        Wd = sb("Wd", [128, KF, D], BF16)
        gb = sb("gb", [128, 2, D], F32)
        bWg, bWu, bWd, bgb = Buf("Wg"), Buf("Wu"), Buf("Wd"), Buf("gb")
        ident, bid = make_ident(k, es, BF16, f"{tag}_id")
        xin = [[sb(f"xin{a}{s}", [128, D], F32) for s in range(2)] for a in range(2)]
        bxin = [[Buf("xin") for s in range(2)] for a in range(2)]
        sxin = [[k.new_sem(f"{tag}_xin") for s in range(2)] for a in range(2)]
        xbf = [[sb(f"xbf{a}{s}", [128, D], BF16) for s in range(2)] for a in range(2)]
        bxbf = [[Buf("xbf") for s in range(2)] for a in range(2)]
        xT = [sb(f"xT{a}", [128, KD, G], BF16) for a in range(2)]
        bxT = [Buf("xT") for a in range(2)]
        ssb = [sb(f"ssb{a}", [128, G], F32) for a in range(2)]
        bssb = [Buf("ssb") for a in range(2)]
        hT = [sb(f"hT{a}", [128, G], BF16) for a in range(3)]
        bhT = [Buf("hT") for a in range(3)]
        y = [sb(f"y{a}", [128, D], F32) for a in range(2)]
        by = [Buf("y") for a in range(2)]
        sy = [k.new_sem(f"{tag}_y") for a in range(2)]
        st = [sb(f"st{a}", [128, 16], F32) for a in range(2)]
        bst = [Buf("st") for a in range(2)]
        pT = [ps(f"pT{a}", [128, KD, 128], BF16) for a in range(2)]
        bpT = [Buf("pT", excl=True) for a in range(2)]
        gu = [ps(f"gu{a}", [128, 2, G], F32) for a in range(2)]
        bgu = [Buf("gu", excl=True) for a in range(2)]
        acc = [[ps(f"acc{s}{h}", [128, 512], F32) for h in range(2)] for s in range(2)]
        bacc = [[Buf("acc", excl=True) for h in range(2)] for s in range(2)]

        swg, swu, swd, sgb = (k.new_sem(f"{tag}_w") for _ in range(4))
        wgv = wg.rearrange("(k p) f -> p k f", p=128)
        wuv = wu.rearrange("(k p) f -> p k f", p=128)
        wdv = wd.rearrange("(c p) n -> p c n", p=128)
        k.dma(k.sp, [(gb[:, 0, :], lng.partition_broadcast(128)), (gb[:, 1, :], lnb.partition_broadcast(128))],
              sgb, W=[bgb])
        k.dma(k.pool, [(Wg[:, kk, :], wgv[:, kk, :]) for kk in range(KD)], swg, W=[bWg])
        k.dma(k.pool, [(Wu[:, kk, :], wuv[:, kk, :]) for kk in range(KD)], swu, W=[bWu])
        k.dma(k.pool, [(Wd[:, c:c + 2, :], wdv[:, c:c + 2, :]) for c in range(0, KF, 2)], swd, W=[bWd])

        def load_x(g):
            a = g % 2
            for s in range(2):
                t = g * 2 + s
                k.dma(k.sp, [(xin[a][s][:], src[t * 128:(t + 1) * 128, :])], sxin[a][s], R=[srcb[t]], W=[bxin[a][s]])

        def prep_x(g):
            a = g % 2
            for s in range(2):
                k.op(k.pool, lambda e: e.tensor_copy(xbf[a][s][:], xin[a][s][:]), R=[bxin[a][s]], W=[bxbf[a][s]])
                for kk in range(KD):
                    k.op(k.pe, lambda e: e.transpose(pT[s][:, kk, :], xbf[a][s][:, kk * 128:(kk + 1) * 128], ident[:]),
                         R=[bxbf[a][s], bid], W=[bpT[s]], sig=(kk == KD - 1))
                k.op(k.act, lambda e: e.copy(xT[a][:, :, s * 128:(s + 1) * 128], pT[s][:]), R=[bpT[s]], W=[bxT[a]])

        def gu_mm(g, c):
            a = g % 2
            sl = c % 2
            for j, (Wt, bW) in enumerate(((Wg, bWg), (Wu, bWu))):
                for kk in range(KD):
                    k.op(k.pe, lambda e: e.matmul(gu[sl][:, j, :], Wt[:, kk, c * 128:(c + 1) * 128], xT[a][:, kk, :],
                                                  start=(kk == 0), stop=(kk == KD - 1)),
                         R=[bW, bxT[a]], W=[bgu[sl]], sig=(j == 1 and kk == KD - 1))

        def act_h(g, c):
            sl = c % 2
            h3 = c % 3
            k.op(k.act, lambda e: e.activation(out=ssb[sl][:], in_=gu[sl][:, 0, :], func=AF.Silu),
                 R=[bgu[sl]], W=[bssb[sl]])
            k.op(k.dve, lambda e: e.scalar_tensor_tensor(out=hT[h3][:], in0=ssb[sl][:], scalar=0.5, in1=gu[sl][:, 1, :],
                                                         op0=ALU.mult, op1=ALU.mult),
                 R=[bssb[sl], bgu[sl]], W=[bhT[h3]])

        def down_mm(g, c):
            h3 = c % 3
            for s in range(2):
                for h in range(2):
                    k.op(k.pe, lambda e: e.matmul(acc[s][h][:], hT[h3][:, s * 128:(s + 1) * 128],
                                                  Wd[:, c, h * 512:(h + 1) * 512], start=(c == 0), stop=(c == KF - 1)),
                         R=[bhT[h3], bWd], W=[bacc[s][h]], sig=(s == 1 and h == 1))

        def epilogue(g):
            a = g % 2
            for s in range(2):
                t = g * 2 + s
                yy, byy, stt, bstt = y[s], by[s], st[s], bst[s]
                for h in range(2):
                    k.op(k.dve, lambda e: e.scalar_tensor_tensor(out=yy[:, h * 512:(h + 1) * 512],
                                                                 in0=xin[a][s][:, h * 512:(h + 1) * 512], scalar=ALPHA,
                                                                 in1=acc[s][h][:], op0=ALU.mult, op1=ALU.add),
                         R=[bxin[a][s], bacc[s][h]], W=[byy])
                layer_norm(k, yy, byy, stt, bstt, gb, bgb)
                k.dma(k.sp, [(dst[t * 128:(t + 1) * 128, :], yy[:])], sy[s], R=[byy], W=[dstb[t]])

        load_x(0)
        if NG > 1:
            load_x(1)
        prep_x(0)
        for g in range(NG):
            gu_mm(g, 0)
            for c in range(KF):
                if c + 1 < KF:
                    gu_mm(g, c + 1)
                elif g + 1 < NG:
                    prep_x(g + 1)
                act_h(g, c)
                down_mm(g, c)
            epilogue(g)
            if g + 2 < NG:
                load_x(g + 2)
        k.barrier()


def layer_norm(k, yy, byy, stt, bstt, gb, bgb):
    for h in range(2):
        k.op(k.dve, lambda e: e.bn_stats(out=stt[:, h * 6:(h + 1) * 6], in_=yy[:, h * 512:(h + 1) * 512]),
             R=[byy], W=[bstt])
    k.op(k.dve, lambda e: e.bn_aggr(out=stt[:, 12:14], in_=stt[:, 0:12]), R=[bstt], W=[bstt])
    k.op(k.dve, lambda e: e.tensor_scalar_add(stt[:, 13:14], stt[:, 13:14], LN_EPS), R=[bstt], W=[bstt])
    k.op(k.act, lambda e: e.activation(out=stt[:, 14:15], in_=stt[:, 13:14], func=AF.Sqrt), R=[bstt], W=[bstt])
    k.op(k.dve, lambda e: e.reciprocal(out=stt[:, 14:15], in_=stt[:, 14:15]), R=[bstt], W=[bstt])
    k.op(k.dve, lambda e: e.scalar_tensor_tensor(out=stt[:, 15:16], in0=stt[:, 12:13], scalar=-1.0, in1=stt[:, 14:15],
                                                 op0=ALU.mult, op1=ALU.mult), R=[bstt], W=[bstt])
    k.op(k.act, lambda e: e.activation(out=yy[:], in_=yy[:], func=AF.Identity, scale=stt[:, 14:15], bias=stt[:, 15:16]),
         R=[byy, bstt], W=[byy])
    k.op(k.pool, lambda e: e.tensor_tensor(out=yy[:], in0=yy[:], in1=gb[:, 0, :], op=ALU.mult), R=[byy, bgb], W=[byy])
    k.op(k.pool, lambda e: e.tensor_tensor(out=yy[:], in0=yy[:], in1=gb[:, 1, :], op=ALU.add), R=[byy, bgb], W=[byy])


def build_ffn_only(T):
    nc = bass.Bass("TRN2", target_bir_lowering=False)
    x = nc.dram_tensor("x", [T, D], F32, kind="ExternalInput").ap()
    wg = nc.dram_tensor("ffn1_w_gate", [D, FF], F32, kind="ExternalInput").ap()
    wu = nc.dram_tensor("ffn1_w_up", [D, FF], F32, kind="ExternalInput").ap()
    wd = nc.dram_tensor("ffn1_w_down", [FF, D], F32, kind="ExternalInput").ap()
    lg = nc.dram_tensor("ln1_g", [1, D], F32, kind="ExternalInput").ap()
    lb = nc.dram_tensor("ln1_b", [1, D], F32, kind="ExternalInput").ap()
    out = nc.dram_tensor("out", [T, D], F32, kind="ExternalOutput").ap()
    k = K(nc)
    srcb = [Buf("x") for _ in range(T // 128)]
    dstb = [Buf("o") for _ in range(T // 128)]
    phase_ffn(k, "f1", x, out, srcb, dstb, wg, wu, wd, lg, lb, T)
    return nc


GT = 256
NCH = GT // 64
C_Z = 1536
C_BETA = 2048
C_QB = 2056
C_KB = 2312
C_VB = 2568
C_GB = 3080
C_LR = 3592
PROJ = 3608
BIG = 1.0e30


class _Stop(Exception):
    pass


def phase_mix(k, tag, src, dst, srcb, dstb, w_in, conv_w, a_log, dt_bias, gdn_nw, w_gk, b_gk, gla_nw, w_out,
              lng, lnb, T, dbg=None, stop=0):
    try:
        _phase_mix(k, tag, src, dst, srcb, dstb, w_in, conv_w, a_log, dt_bias, gdn_nw, w_gk, b_gk, gla_nw, w_out,
                   lng, lnb, T, dbg, stop)
    except _Stop:
        pass
    k.barrier()


def _phase_mix(k, tag, src, dst, srcb, dstb, w_in, conv_w, a_log, dt_bias, gdn_nw, w_gk, b_gk, gla_nw, w_out,
               lng, lnb, T, dbg, stop):
    nc = k.nc
    NG = T // GT

    def chk(n):
        if stop == n:
            raise _Stop()

    with ExitStack() as es:
        k.begin_phase(es, tag)
        pe, act, dve, pool, sp = k.pe, k.act, k.dve, k.pool, k.sp

        def sb(name, shape, dt=F32):
            return es.enter_context(nc.sbuf_tensor(f"{tag}_{name}", shape, dt)), Buf(name)

        banks = [es.enter_context(nc.psum_tensor(f"{tag}_bk{i}", [128, 512], F32)) for i in range(7)]
        pTb = es.enter_context(nc.psum_tensor(f"{tag}_pT", [128, KD, 128], BF16))
        bpT = Buf("pT", excl=True)
        bbank = [Buf(f"bk{i}", excl=True) for i in range(7)]
        cnt = {"r": 0}

        def P2():
            i = cnt["r"] % 7
            cnt["r"] += 1
            return banks[i], bbank[i]

        def P1():
            b, bb = P2()
            return b, 0, bb

        Win, bWin = sb("Win", [128, KD, PROJ], BF16)
        Wo, bWo = sb("Wo", [128, KD, D], BF16)
        gbt, bgb = sb("gb", [128, 2, D])
        cw, bcw = sb("cw", [128, 12, 4])
        nw, bnw = sb("nw", [128, 2])
        wgk, bwgk = sb("wgk", [16, 256])
        bgk, bbgk = sb("bgk", [1, 256])
        hc, bhc = sb("hc", [64, 2, 4])
        hcc, bhcc = sb("hcc", [64, 2, NCH, 4])
        identb, bidb = make_ident(k, es, BF16, f"{tag}_idb")
        identf, bidf = make_ident(k, es, F32, f"{tag}_idf")
        cU, bcU = sb("cU", [64, 4, 64])
        cUn, bcUn = sb("cUn", [64, 64])
        cSLn, bcSLn = sb("cSLn", [64, 64])
        cMA, bcMA = sb("cMA", [64, 4, 64])
        cMQ, bcMQ = sb("cMQ", [64, 4, 64])
        cR0, bcR0 = sb("cR0", [64, 4, 64])
        ones, bones = sb("ones", [128, 128])
        nones, bnones = sb("nones", [64, 64])
        CONST = [bcU, bcUn, bcSLn, bcMA, bcMQ, bcR0, bones, bnones, bidf, bidb]

        sW = [k.new_sem(f"{tag}_w") for _ in range(3)]
        winv = w_in.rearrange("(k p) f -> p k f", p=128)
        wov = w_out.rearrange("(k p) f -> p k f", p=128)
        k.dma(pool, [(Win[:, kk, :], winv[:, kk, :]) for kk in range(KD)], sW[0], W=[bWin])
        k.dma(pool, [(Wo[:, kk, :], wov[:, kk, :]) for kk in range(KD)], sW[1], W=[bWo])
        with nc.allow_non_contiguous_dma(reason="tiny constant loads"):
            k.dma(sp, [(gbt[:, 0, :], lng.partition_broadcast(128)), (gbt[:, 1, :], lnb.partition_broadcast(128)),
                       (nw[:, 0:1], gdn_nw.rearrange("o (p u) -> (o p) u", u=1)),
                       (nw[:, 1:2], gla_nw.rearrange("o (p u) -> (o p) u", u=1)),
                       (wgk[:], w_gk), (bgk[:], b_gk),
                       (hc[:, 0, :], a_log.partition_broadcast(64)), (hc[:, 1, :], dt_bias.partition_broadcast(64))],
                  sW[2], W=[bgb, bnw, bwgk, bbgk, bhc])

        def psel(t, b, base_val, pattern, cm, cmp, fill):
            k.op(pool, lambda e: e.memset(t, base_val), W=[b])
            k.op(pool, lambda e: e.affine_select(out=t, in_=t, pattern=pattern, compare_op=cmp, fill=fill, base=0,
                                                 channel_multiplier=cm), R=[b], W=[b])

        psel(cU[:], bcU, 1.0, [[0, 4], [1, 64]], -1, ALU.is_ge, 0.0)
        psel(cUn[:], bcUn, -1.0 / 16, [[1, 64]], -1, ALU.is_ge, 0.0)
        psel(cSLn[:], bcSLn, -1.0 / 16, [[-1, 64]], 1, ALU.is_gt, 0.0)
        psel(cMA[:], bcMA, 0.0, [[0, 4], [-1, 64]], 1, ALU.is_gt, -BIG)
        psel(cMQ[:], bcMQ, 0.0, [[0, 4], [1, 64]], -1, ALU.is_ge, BIG)
        psel(cR0[:], bcR0, 0.0, [[0, 4], [-1, 64]], 1, ALU.not_equal, 1.0)
        k.op(pool, lambda e: e.memset(ones[:], 1.0), W=[bones])
        k.op(pool, lambda e: e.memset(nones[:], -1.0), W=[bnones])
        k.op(act, lambda e: e.activation(out=hc[:, 0, :], in_=hc[:, 0, :], func=AF.Exp), R=[bhc], W=[bhc])
        for c in range(NCH):
            k.op(dve, lambda e: e.tensor_scalar_mul(hcc[:, 0, c, :], hc[:, 0, :], -1.0), R=[bhc], W=[bhcc])
            k.op(dve, lambda e: e.tensor_copy(hcc[:, 1, c, :], hc[:, 1, :]), R=[bhc], W=[bhcc])

        xin = [sb(f"xin{s}", [128, D]) for s in range(2)]
        sxin = [k.new_sem(f"{tag}_xin") for s in range(2)]
        xbf = [sb(f"xbf{s}", [128, D], BF16) for s in range(2)]
        xT, bxT = sb("xT", [128, KD, GT], BF16)
        cb = [sb(f"cb{t}", [128, GT + 3]) for t in range(12)]
        cv, bcv = sb("cv", [128, GT])
        cs = [sb(f"cs{t}", [128, GT]) for t in range(12)]
        sq, bsq = cv, bcv
        rn, brn = sb("rn", [128, GT])
        qkb = [sb(f"qkb{t}", [128, GT], BF16) for t in range(8)]
        sz = [sb(f"sz{t}", [128, GT], BF16) for t in range(8)]
        qbT = [sb(f"qbT{p}", [64, GT]) for p in range(4)]
        kbT = [sb(f"kbT{p}", [64, GT]) for p in range(4)]
        lrT, blrT = sb("lrT", [16, GT])
        kbt, bkbt = sb("kbt", [64, NCH, 256])
        vbt, bvbt = sb("vbt", [64, NCH, 512])
        cwr, bcwr = vbt[0:4, 0:3, :].rearrange("p a b -> p (a b)"), bvbt
        bd, bbd = sb("bd", [64, NCH, 8])
        beta, bbeta = sb("beta", [64, NCH, 4])
        nbeta, bnbeta = sb("nbeta", [64, NCH, 4])
        gtok, bgtok = sb("gtok", [64, NCH, 4])
        tmp8, btmp8 = sb("tmp8", [64, NCH, 4])
        oT, boT = sb("oT", [128, 8, GT])
        mixT, bmixT = sb("mixT", [128, 8, GT], BF16)
        sy = [k.new_sem(f"{tag}_y") for s in range(2)]
        st = [sb(f"st{s}", [128, 16]) for s in range(2)]
        Sg, bSg = sb("Sg", [128, 4, 128])
        Sl, bSl = sb("Sl", [64, 4, 128])
        gU, bgU = sb("gU", [64, 4, 64])
        tA, btA = sb("tA", [64, 4, 64])
        tQ, btQ = sb("tQ", [64, 4, 64])
        EGb, bEGb = sb("EGb", [128, 4, 64])
        sc, bsc = sb("sc", [64, 12])
        M = [sb(f"M{i}", [64, 4, 64]) for i in range(2)]
        N = [sb(f"N{i}", [64, 4, 64]) for i in range(2)]
        Rm = [sb(f"R{i}", [64, 4, 64]) for i in range(2)]
        qkd, bqkd = sb("qkd", [64, 4, 64])
        bk_, bbk = sb("bk", [64, 4, 128])
        kd_, bkd = sb("kd", [64, 4, 128])
        bv_, bbv = sb("bv", [64, 4, 128])
        wT, bwT = sb("wT", [128, 4, 64])
        u_, bu = sb("u", [64, 4, 128])
        qd, bqd = sb("qd", [128, 4, 64])
        vn, bvn = sb("vn", [64, 4, 128])
        lt, blt = sb("lt", [64, 256])
        ebT, bebT = sb("ebT", [64, 4, 64])
        enbT, benbT = sb("enbT", [64, 4, 64])
        qt, bqt = sb("qt", [64, 4, 64])
        kt, bkt = sb("kt", [64, 4, 64])
        att, batt = tA, btA
        kdec, bkdec = sb("kdec", [64, 256])

        k.dma(sp, [(cwr, conv_w)], k.new_sem(f"{tag}_cw"), W=[bcwr])
        for t in range(12):
            bank, off, pb = P1()
            k.op(pe, lambda e: e.transpose(bank[:, off:off + 4], cwr[:, t * 128:(t + 1) * 128], identf[0:4, 0:4]),
                 R=[bcwr, bidf], W=[pb])
            k.op(act, lambda e: e.copy(cw[:, t, :], bank[:, off:off + 4]), R=[pb], W=[bcw])
        k.op(pool, lambda e: e.memset(Sg[:], 0.0), W=[bSg])
        k.op(pool, lambda e: e.memset(Sl[:], 0.0), W=[bSl])
        for t in range(12):
            k.op(pool, lambda e: e.memset(cb[t][0][:, 0:3], 0.0), W=[cb[t][1]])

        def v3(bank, off, p, a, b_):
            return bank[0:p, off:off + a * b_].rearrange("p (a b) -> p a b", a=a)

        sdbg = k.new_sem(f"{tag}_dbg")
        chk(1)
        for g in range(NG):
            for s in range(2):
                t = g * 2 + s
                k.dma(sp, [(xin[s][0][:], src[t * 128:(t + 1) * 128, :])], sxin[s], R=[srcb[t]], W=[xin[s][1]])
                k.op(pool, lambda e: e.tensor_copy(xbf[s][0][:], xin[s][0][:]), R=[xin[s][1]], W=[xbf[s][1]])
                for kk in range(KD):
                    k.op(pe, lambda e: e.transpose(pTb[:, kk, :], xbf[s][0][:, kk * 128:(kk + 1) * 128], identb[:]),
                         R=[xbf[s][1], bidb], W=[bpT], sig=(kk == KD - 1))
                k.op(act, lambda e: e.copy(xT[:, :, s * 128:(s + 1) * 128], pTb[:]), R=[bpT], W=[bxT])

            def proj_fm(col, m):
                bank, off, pb = P1()
                for kk in range(KD):
                    k.op(pe, lambda e: e.matmul(bank[0:m, off:off + GT], Win[:, kk, col:col + m], xT[:, kk, :],
                                                start=(kk == 0), stop=(kk == KD - 1)),
                         R=[bWin, bxT], W=[pb], sig=(kk == KD - 1))
                return bank[0:m, off:off + GT], pb

            for t in range(12):
                ps_, pb = proj_fm(t * 128, 128)
                cbt, bcb = cb[t]
                k.op(act, lambda e: e.copy(cbt[:, 3:3 + GT], ps_), R=[pb], W=[bcb])
                k.op(pool, lambda e: e.tensor_scalar_mul(cv[:], cbt[:, 0:GT], cw[:, t, 0:1]), R=[bcb, bcw], W=[bcv])
                for j in range(1, 4):
                    k.op(dve, lambda e: e.scalar_tensor_tensor(out=cv[:], in0=cbt[:, j:j + GT], scalar=cw[:, t, j:j + 1],
                                                                in1=cv[:], op0=ALU.mult, op1=ALU.add),
                         R=[bcb, bcw, bcv], W=[bcv])
                k.op(pool, lambda e: e.tensor_copy(cbt[:, 0:3], cbt[:, GT:GT + 3]), R=[bcb], W=[bcb])
                k.op(act, lambda e: e.activation(out=cs[t][0][:], in_=cv[:], func=AF.Silu), R=[bcv], W=[cs[t][1]])
            for t in range(8):
                col = (C_Z + t * 128) if t < 4 else (C_GB + (t - 4) * 128)
                ps_, pb = proj_fm(col, 128)
                k.op(act, lambda e: e.activation(out=sz[t][0][:], in_=ps_, func=AF.Silu), R=[pb], W=[sz[t][1]])
            for p in range(4):
                ps_, pb = proj_fm(C_QB + p * 64, 64)
                k.op(act, lambda e: e.copy(qbT[p][0][:], ps_), R=[pb], W=[qbT[p][1]])
                ps_, pb = proj_fm(C_KB + p * 64, 64)
                k.op(act, lambda e: e.copy(kbT[p][0][:], ps_), R=[pb], W=[kbT[p][1]])
            ps_, pb = proj_fm(C_LR, 16)
            k.op(act, lambda e: e.copy(lrT[:], ps_), R=[pb], W=[blrT])
            for c in range(NCH):
                tok = slice(c * 64, (c + 1) * 64)
                bank, off, pb = P1()
                for kk in range(KD):
                    k.op(pe, lambda e: e.matmul(bank[0:64, off:off + 8], xT[:, kk, tok], Win[:, kk, C_BETA:C_BETA + 8],
                                                start=(kk == 0), stop=(kk == KD - 1)),
                         R=[bWin, bxT], W=[pb], sig=(kk == KD - 1))
                k.op(act, lambda e: e.copy(bd[:, c, :], bank[0:64, off:off + 8]), R=[pb], W=[bbd])
                bank, off, pb = P1()
                for kk in range(KD):
                    k.op(pe, lambda e: e.matmul(bank[0:64, off:off + 256], xT[:, kk, tok], Win[:, kk, C_KB:C_KB + 256],
                                                start=(kk == 0), stop=(kk == KD - 1)),
                         R=[bWin, bxT], W=[pb], sig=(kk == KD - 1))
                k.op(act, lambda e: e.copy(kbt[:, c, :], bank[0:64, off:off + 256]), R=[pb], W=[bkbt])
                bank2, pb2 = P2()
                for kk in range(KD):
                    k.op(pe, lambda e: e.matmul(bank2[0:64, :], xT[:, kk, tok], Win[:, kk, C_VB:C_VB + 512],
                                                start=(kk == 0), stop=(kk == KD - 1)),
                         R=[bWin, bxT], W=[pb2], sig=(kk == KD - 1))
                k.op(act, lambda e: e.copy(vbt[:, c, :], bank2[0:64, :]), R=[pb2], W=[bvbt])
            chk(2)
            for t in range(8):
                cst, bcs = cs[t]
                k.op(pool, lambda e: e.tensor_tensor(out=sq[:], in0=cst[:], in1=cst[:], op=ALU.mult), R=[bcs], W=[bsq])
                bank, off, pb = P1()
                k.op(pe, lambda e: e.matmul(bank[:, off:off + GT], ones[:], sq[:], start=True, stop=True),
                     R=[bones, bsq], W=[pb])
                k.op(act, lambda e: e.activation(out=rn[:], in_=bank[:, off:off + GT], func=AF.Sqrt, bias=RMS_EPS),
                     R=[pb], W=[brn])
                k.op(dve, lambda e: e.reciprocal(out=rn[:], in_=rn[:]), R=[brn], W=[brn])
                scl = (128.0 ** -0.5) if t < 4 else 1.0
                k.op(dve, lambda e: e.scalar_tensor_tensor(out=cst[:], in0=cst[:], scalar=scl, in1=rn[:], op0=ALU.mult,
                                                           op1=ALU.mult), R=[bcs, brn], W=[bcs])
                k.op(pool, lambda e: e.tensor_copy(qkb[t][0][:], cst[:]), R=[bcs], W=[qkb[t][1]])
            k.op(act, lambda e: e.activation(out=beta[:], in_=bd[:, :, 0:4], func=AF.Sigmoid), R=[bbd], W=[bbeta])
            k.op(dve, lambda e: e.tensor_scalar_mul(nbeta[:], beta[:], -1.0), R=[bbeta], W=[bnbeta])
            k.op(dve, lambda e: e.tensor_tensor(out=tmp8[:], in0=bd[:, :, 4:8], in1=hcc[:, 1, :, :], op=ALU.add),
                 R=[bbd, bhcc], W=[btmp8])
            k.op(act, lambda e: e.activation(out=tmp8[:], in_=tmp8[:], func=AF.Exp), R=[btmp8], W=[btmp8])
            k.op(act, lambda e: e.activation(out=tmp8[:], in_=tmp8[:], func=AF.Ln, bias=1.0), R=[btmp8], W=[btmp8])
            k.op(dve, lambda e: e.tensor_tensor(out=gtok[:], in0=tmp8[:], in1=hcc[:, 0, :, :], op=ALU.mult),
                 R=[btmp8, bhcc], W=[bgtok])

            chk(3)
            for c in range(NCH):
                tok = slice(c * 64, (c + 1) * 64)
                for h in range(4):
                    k.op(dve, lambda e: e.tensor_scalar_mul(gU[:, h, :], cU[:, h, :], gtok[:, c, h:h + 1]),
                         R=[bcU, bgtok], W=[bgU])
                bank, off, pG = P1()
                k.op(pe, lambda e: e.matmul(bank[0:64, off:off + 4], cU[:, 0, :], gtok[:, c, :], start=True, stop=True),
                     R=[bcU, bgtok], W=[pG])
                k.op(act, lambda e: e.activation(out=sc[:, 0:4], in_=bank[0:64, off:off + 4], func=AF.Exp), R=[pG], W=[bsc])
                k.op(dve, lambda e: e.tensor_tensor(out=sc[:, 4:8], in0=sc[:, 0:4], in1=beta[:, c, :], op=ALU.mult),
                     R=[bsc, bbeta], W=[bsc])
                bankD, offD, pD = P1()
                Dps = v3(bankD, offD, 64, 4, 64)
                for h in range(4):
                    k.op(pe, lambda e: e.matmul(Dps[:, h, :], nones[:], gU[:, h, :], start=True, stop=False),
                         R=[bnones, bgU], W=[pD], sig=False)
                    k.op(pe, lambda e: e.matmul(Dps[:, h, :], gU[:, h, :], ones[0:64, 0:64], start=False, stop=True),
                         R=[bones, bgU], W=[pD], sig=(h == 3))
                bankG, offG, pGb = P1()
                k.op(pe, lambda e: e.matmul(bankG[:, offG:offG + 256], ones[0:64, :], gU[:].rearrange("p a b -> p (a b)"),
                                            start=True, stop=True), R=[bones, bgU], W=[pGb])
                k.op(dve, lambda e: e.tensor_tensor(out=tA[:], in0=Dps, in1=cMA[:], op=ALU.add), R=[pD, bcMA], W=[btA])
                k.op(dve, lambda e: e.tensor_tensor(out=tQ[:], in0=Dps, in1=cMQ[:], op=ALU.add), R=[pD, bcMQ], W=[btQ])
                k.op(act, lambda e: e.activation(out=sc[:, 8:12], in_=Dps[:, :, 63], func=AF.Exp, scale=-1.0),
                     R=[pD], W=[bsc])
                k.op(act, lambda e: e.activation(out=tA[:], in_=tA[:], func=AF.Exp), R=[btA], W=[btA])
                k.op(act, lambda e: e.activation(out=tQ[:], in_=tQ[:], func=AF.Exp, scale=-1.0), R=[btQ], W=[btQ])
                k.op(act, lambda e: e.activation(out=EGb[:].rearrange("p a b -> p (a b)"), in_=bankG[:, offG:offG + 256],
                                                 func=AF.Exp), R=[pGb], W=[bEGb])
                chk(31)
                bankK, offK, pK = P1()
                kkps = v3(bankK, offK, 64, 4, 64)
                bankQ, offQ, pQ = P1()
                qkps = v3(bankQ, offQ, 64, 4, 64)
                for h in range(4):
                    k.op(pe, lambda e: e.matmul(kkps[:, h, :], qkb[4 + h][0][:, tok], qkb[4 + h][0][:, tok], start=True,
                                                stop=True), R=[qkb[4 + h][1]], W=[pK], sig=(h == 3))
                for h in range(4):
                    k.op(pe, lambda e: e.matmul(qkps[:, h, :], qkb[4 + h][0][:, tok], qkb[h][0][:, tok], start=True,
                                                stop=True), R=[qkb[4 + h][1], qkb[h][1]], W=[pQ], sig=(h == 3))
                M0, bM0 = M[0]
                for h in range(4):
                    k.op(dve, lambda e: e.scalar_tensor_tensor(out=M0[:, h, :], in0=kkps[:, h, :], scalar=nbeta[:, c, h:h + 1],
                                                               in1=tA[:, h, :], op0=ALU.mult, op1=ALU.mult),
                         R=[pK, bnbeta, btA], W=[bM0])
                k.op(dve, lambda e: e.tensor_tensor(out=qkd[:], in0=qkps, in1=tQ[:], op=ALU.mult), R=[pQ, btQ], W=[bqkd])
                chk(32)
                bankN, offN, pN = P1()
                Nps = v3(bankN, offN, 64, 4, 64)
                for h in range(4):
                    k.op(pe, lambda e: e.matmul(Nps[:, h, :], M0[:, h, :], identf[0:64, 0:64], start=True, stop=True),
                         R=[bM0, bidf], W=[pN], sig=(h == 3))
                N0, bN0 = N[0]
                k.op(act, lambda e: e.copy(N0[:], Nps), R=[pN], W=[bN0])
                k.op(dve, lambda e: e.tensor_tensor(out=Rm[0][0][:], in0=Nps, in1=cR0[:], op=ALU.add), R=[pN, bcR0],
                     W=[Rm[0][1]])
                chk(33)
                for i in range(5):
                    Mi, bMi = M[i % 2]
                    Mn, bMn = M[(i + 1) % 2]
                    Ni, bNi = N[i % 2]
                    Nn, bNn = N[(i + 1) % 2]
                    Ri, bRi = Rm[i % 2]
                    Rn_, bRn = Rm[(i + 1) % 2]
                    bankM, offM, pM = P1()
                    Mps = v3(bankM, offM, 64, 4, 64)
                    for h in range(4):
                        k.op(pe, lambda e: e.matmul(Mps[:, h, :], Ni[:, h, :], Mi[:, h, :], start=True, stop=True),
                             R=[bNi, bMi], W=[pM], sig=(h == 3))
                    if i < 4:
                        bankN2, offN2, pN2 = P1()
                        Nps2 = v3(bankN2, offN2, 64, 4, 64)
                        for h in range(4):
                            k.op(pe, lambda e: e.matmul(Nps2[:, h, :], Mi[:, h, :], Ni[:, h, :], start=True, stop=True),
                                 R=[bNi, bMi], W=[pN2], sig=(h == 3))
                    k.op(act, lambda e: e.copy(Mn[:], Mps), R=[pM], W=[bMn])
                    if i < 4:
                        k.op(dve, lambda e: e.tensor_copy(Nn[:], Nps2), R=[pN2], W=[bNn])
                    bankR, offR, pR = P1()
                    Rps = v3(bankR, offR, 64, 4, 64)
                    for h in range(4):
                        k.op(pe, lambda e: e.matmul(Rps[:, h, :], Mn[:, h, :], Ri[:, h, :], start=True, stop=True),
                             R=[bMn, bRi], W=[pR], sig=(h == 3))
                    k.op(dve, lambda e: e.tensor_tensor(out=Rn_[:], in0=Rps, in1=Ri[:], op=ALU.add), R=[pR, bRi], W=[bRn])
                Rf, bRf = Rm[1]
                chk(34)
                bankKt, pKt = P2()
                ktps = v3(bankKt, 0, 64, 4, 128)
                bankVt, pVt = P2()
                vtps = v3(bankVt, 0, 64, 4, 128)
                for h in range(4):
                    k.op(pe, lambda e: e.matmul(ktps[:, h, :], cs[4 + h][0][:, tok], identf[:], start=True, stop=True),
                         R=[cs[4 + h][1], bidf], W=[pKt], sig=(h == 3))
                for h in range(4):
                    k.op(pe, lambda e: e.matmul(vtps[:, h, :], cs[8 + h][0][:, tok], identf[:], start=True, stop=True),
                         R=[cs[8 + h][1], bidf], W=[pVt], sig=(h == 3))
                for h in range(4):
                    k.op(act, lambda e: e.activation(out=bk_[:, h, :], in_=ktps[:, h, :], func=AF.Identity, scale=sc[:, 4 + h:5 + h]),
                         R=[pKt, bsc], W=[bbk])
                    k.op(act, lambda e: e.activation(out=kd_[:, h, :], in_=ktps[:, h, :], func=AF.Identity, scale=sc[:, 8 + h:9 + h]),
                         R=[pKt, bsc], W=[bkd])
                    k.op(act, lambda e: e.activation(out=bv_[:, h, :], in_=vtps[:, h, :], func=AF.Identity, scale=beta[:, c, h:h + 1]),
                         R=[pVt, bbeta], W=[bbv])
                chk(35)
                bankW, offW, pW = P1()
                wps = v3(bankW, offW, 128, 4, 64)
                for h in range(4):
                    k.op(pe, lambda e: e.matmul(wps[:, h, :], bk_[:, h, :], Rf[:, h, :], start=True, stop=True),
                         R=[bbk, bRf], W=[pW], sig=(h == 3))
                k.op(act, lambda e: e.copy(wT[:], wps), R=[pW], W=[bwT])
                bankU, pU = P2()
                ups = v3(bankU, 0, 64, 4, 128)
                for h in range(4):
                    k.op(pe, lambda e: e.matmul(ups[:, h, :], Rf[:, h, :], bv_[:, h, :], start=True, stop=True),
                         R=[bbv, bRf], W=[pU], sig=(h == 3))
                k.op(act, lambda e: e.copy(u_[:], ups), R=[pU], W=[bu])
                for h in range(4):
                    k.op(pool, lambda e: e.tensor_tensor(out=qd[:, h, :], in0=cs[h][0][:, tok], in1=EGb[:, h, :], op=ALU.mult),
                         R=[cs[h][1], bEGb], W=[bqd])
                chk(4)
                bankWS, pWS = P2()
                wsps = v3(bankWS, 0, 64, 4, 128)
                for h in range(4):
                    k.op(pe, lambda e: e.matmul(wsps[:, h, :], wT[:, h, :], Sg[:, h, :], start=True, stop=True),
                         R=[bwT, bSg], W=[pWS], sig=(h == 3))
                k.op(dve, lambda e: e.tensor_tensor(out=vn[:], in0=u_[:], in1=wsps, op=ALU.subtract), R=[bu, pWS], W=[bvn])
                bankO, offO, pO = P1()
                ops_ = v3(bankO, offO, 128, 4, 64)
                for h in range(4):
                    k.op(pe, lambda e: e.matmul(ops_[:, h, :], Sg[:, h, :], qd[:, h, :], start=True, stop=False),
                         R=[bSg, bqd], W=[pO], sig=False)
                    k.op(pe, lambda e: e.matmul(ops_[:, h, :], vn[:, h, :], qkd[:, h, :], start=False, stop=True),
                         R=[bvn, bqkd], W=[pO], sig=(h == 3))
                k.op(act, lambda e: e.copy(oT[:, 0:4, tok], ops_), R=[pO], W=[boT])
                bankS, pS = P2()
                dsps = v3(bankS, 0, 128, 4, 128)
                for h in range(4):
                    k.op(pe, lambda e: e.matmul(dsps[:, h, :], kd_[:, h, :], vn[:, h, :], start=True, stop=True),
                         R=[bkd, bvn], W=[pS], sig=(h == 3))
                for h in range(4):
                    k.op(dve, lambda e: e.scalar_tensor_tensor(out=Sg[:, h, :], in0=Sg[:, h, :], scalar=EGb[:, h, 63:64],
                                                               in1=dsps[:, h, :], op0=ALU.mult, op1=ALU.add),
                         R=[bSg, bEGb, pS], W=[bSg])

                chk(5)
                bankZ, offZ, pZ = P1()
                k.op(pe, lambda e: e.matmul(bankZ[0:64, offZ:offZ + 256], lrT[:, tok], wgk[:], start=True, stop=False),
                     R=[blrT, bwgk], W=[pZ], sig=False)
                k.op(pe, lambda e: e.matmul(bankZ[0:64, offZ:offZ + 256], ones[0:1, 0:64], bgk[:], start=False, stop=True),
                     R=[bones, bbgk], W=[pZ])
                k.op(act, lambda e: e.activation(out=lt[:], in_=bankZ[0:64, offZ:offZ + 256], func=AF.Exp, scale=-1.0),
                     R=[pZ], W=[blt])
                k.op(act, lambda e: e.activation(out=lt[:], in_=lt[:], func=AF.Ln, bias=1.0), R=[blt], W=[blt])
                bankB, offB, pB = P1()
                bps = v3(bankB, offB, 64, 4, 64)
                for h in range(4):
                    k.op(pe, lambda e: e.matmul(bps[:, h, :], lt[:, h * 64:(h + 1) * 64], cUn[:], start=True, stop=True),
                         R=[blt, bcUn], W=[pB], sig=(h == 3))
                bankE, offE, pE = P1()
                k.op(pe, lambda e: e.matmul(bankE[0:64, offE:offE + 256], cSLn[:], lt[:], start=True, stop=True),
                     R=[blt, bcSLn], W=[pE])
                k.op(act, lambda e: e.activation(out=ebT[:], in_=bps, func=AF.Exp), R=[pB], W=[bebT])
                k.op(act, lambda e: e.activation(out=enbT[:], in_=bps, func=AF.Exp, scale=-1.0), R=[pB], W=[benbT])
                k.op(act, lambda e: e.activation(out=kdec[:], in_=bankE[0:64, offE:offE + 256], func=AF.Exp), R=[pE], W=[bkdec])
                k.op(pool, lambda e: e.tensor_tensor(out=kdec[:], in0=kdec[:], in1=kbt[:, c, :], op=ALU.mult),
                     R=[bkdec, bkbt], W=[bkdec])
                for h in range(4):
                    k.op(dve, lambda e: e.scalar_tensor_tensor(out=qt[:, h, :], in0=qbT[h][0][:, tok], scalar=0.125,
                                                               in1=ebT[:, h, :], op0=ALU.mult, op1=ALU.mult),
                         R=[qbT[h][1], bebT], W=[bqt])
                    k.op(pool, lambda e: e.tensor_tensor(out=kt[:, h, :], in0=kbT[h][0][:, tok], in1=enbT[:, h, :], op=ALU.mult),
                         R=[kbT[h][1], benbT], W=[bkt])
                bankA, offA, pA = P1()
                aps = v3(bankA, offA, 64, 4, 64)
                for h in range(4):
                    k.op(pe, lambda e: e.matmul(aps[:, h, :], kt[:, h, :], qt[:, h, :], start=True, stop=True),
                         R=[bkt, bqt], W=[pA], sig=(h == 3))
                k.op(dve, lambda e: e.tensor_tensor(out=att[:], in0=aps, in1=cU[:], op=ALU.mult), R=[pA, bcU], W=[batt])
                bankO2, offO2, pO2 = P1()
                ops2 = v3(bankO2, offO2, 128, 4, 64)
                for h in range(4):
                    k.op(pe, lambda e: e.matmul(ops2[:, h, :], Sl[:, h, :], qt[:, h, :], start=True, stop=False),
                         R=[bSl, bqt], W=[pO2], sig=False)
                    k.op(pe, lambda e: e.matmul(ops2[:, h, :], vbt[:, c, h * 128:(h + 1) * 128], att[:, h, :], start=False, stop=True),
                         R=[bvbt, batt], W=[pO2], sig=(h == 3))
                k.op(act, lambda e: e.copy(oT[:, 4:8, tok], ops2), R=[pO2], W=[boT])
                bankS2, pS2 = P2()
                ds2 = v3(bankS2, 0, 64, 4, 128)
                for h in range(4):
                    k.op(pe, lambda e: e.matmul(ds2[:, h, :], kdec[:, h * 64:(h + 1) * 64], vbt[:, c, h * 128:(h + 1) * 128],
                                                start=True, stop=True), R=[bkdec, bvbt], W=[pS2], sig=(h == 3))
                for h in range(4):
                    k.op(dve, lambda e: e.scalar_tensor_tensor(out=Sl[:, h, :], in0=Sl[:, h, :], scalar=ebT[:, h, 63:64],
                                                               in1=ds2[:, h, :], op0=ALU.mult, op1=ALU.add),
                         R=[bSl, bebT, pS2], W=[bSl])

            chk(6)
            for h in range(8):
                k.op(pool, lambda e: e.tensor_tensor(out=sq[:], in0=oT[:, h, :], in1=oT[:, h, :], op=ALU.mult), R=[boT], W=[bsq])
                bank, off, pb = P1()
                k.op(pe, lambda e: e.matmul(bank[:, off:off + GT], ones[:], sq[:], start=True, stop=True), R=[bones, bsq], W=[pb])
                k.op(act, lambda e: e.activation(out=rn[:], in_=bank[:, off:off + GT], func=AF.Sqrt, scale=1.0 / 128, bias=RMS_EPS),
                     R=[pb], W=[brn])
                k.op(dve, lambda e: e.reciprocal(out=rn[:], in_=rn[:]), R=[brn], W=[brn])
                wcol = nw[:, 0:1] if h < 4 else nw[:, 1:2]
                k.op(dve, lambda e: e.scalar_tensor_tensor(out=rn[:], in0=oT[:, h, :], scalar=wcol, in1=rn[:], op0=ALU.mult,
                                                           op1=ALU.mult), R=[boT, bnw, brn], W=[brn])
                k.op(pool, lambda e: e.tensor_tensor(out=mixT[:, h, :], in0=rn[:], in1=sz[h][0][:], op=ALU.mult),
                     R=[brn, sz[h][1]], W=[bmixT])
            if dbg is not None:
                k.dma(sp, [(dbg[:, :, g * GT:(g + 1) * GT], oT[:])], sdbg, R=[boT], W=[])
            for s in range(2):
                t = g * 2 + s
                yy, byy = xin[s]
                for hf in range(2):
                    bank2, pb2 = P2()
                    for mc in range(8):
                        k.op(pe, lambda e: e.matmul(bank2[:, :], mixT[:, mc, s * 128:(s + 1) * 128], Wo[:, mc, hf * 512:(hf + 1) * 512],
                                                    start=(mc == 0), stop=(mc == 7)), R=[bmixT, bWo], W=[pb2], sig=(mc == 7))
                    k.op(dve, lambda e: e.scalar_tensor_tensor(out=yy[:, hf * 512:(hf + 1) * 512], in0=xin[s][0][:, hf * 512:(hf + 1) * 512],
                                                               scalar=ALPHA, in1=bank2[:, :], op0=ALU.mult, op1=ALU.add),
                         R=[pb2], W=[byy])
                layer_norm(k, yy, byy, st[s][0], st[s][1], gbt, bgb)
                k.dma(sp, [(dst[t * 128:(t + 1) * 128, :], yy[:])], sy[s], R=[byy], W=[dstb[t]])
        k.barrier()


def build_mix_only(T, stop=0):
    nc = bass.Bass("TRN2", target_bir_lowering=False)
    def inp(name, shape):
        return nc.dram_tensor(name, shape, F32, kind="ExternalInput").ap()
    x1 = inp("x1", [T, D])
    w_in = inp("w_in", [D, PROJ]); conv_w = inp("conv_w", [4, 1536]); a_log = inp("a_log", [1, 4])
    dt_bias = inp("dt_bias", [1, 4]); gdn_nw = inp("gdn_norm_w", [1, 128]); w_gk = inp("w_gk", [16, 256])
    b_gk = inp("b_gk", [1, 256]); gla_nw = inp("gla_norm_w", [1, 128]); w_out = inp("w_out", [D, D])
    lg = inp("ln2_g", [1, D]); lb = inp("ln2_b", [1, D])
    out = nc.dram_tensor("out", [T, D], F32, kind="ExternalOutput").ap()
    dbg = nc.dram_tensor("dbg", [128, 8, T], F32, kind="ExternalOutput").ap()
    k = K(nc)
    srcb = [Buf("x") for _ in range(T // 128)]
    dstb = [Buf("o") for _ in range(T // 128)]
    phase_mix(k, "mx", x1, out, srcb, dstb, w_in, conv_w, a_log, dt_bias, gdn_nw, w_gk, b_gk, gla_nw, w_out, lg, lb, T, dbg=dbg, stop=stop)
    return nc


W_NAMES = [("ffn1_w_gate", [D, FF]), ("ffn1_w_up", [D, FF]), ("ffn1_w_down", [FF, D]), ("ln1_g", [1, D]), ("ln1_b", [1, D]),
           ("w_in", [D, PROJ]), ("conv_w", [4, 1536]), ("a_log", [1, 4]), ("dt_bias", [1, 4]), ("gdn_norm_w", [1, 128]),
           ("w_gk", [16, 256]), ("b_gk", [1, 256]), ("gla_norm_w", [1, 128]), ("w_out", [D, D]),
           ("ln2_g", [1, D]), ("ln2_b", [1, D]),
           ("ffn2_w_gate", [D, FF]), ("ffn2_w_up", [D, FF]), ("ffn2_w_down", [FF, D]), ("ln3_g", [1, D]), ("ln3_b", [1, D])]


def build_full(T):
    nc = bass.Bass("TRN2", target_bir_lowering=False)
    x = nc.dram_tensor("x", [T, D], F32, kind="ExternalInput").ap()
    w = {n: nc.dram_tensor(n, shp, F32, kind="ExternalInput").ap() for n, shp in W_NAMES}
    out = nc.dram_tensor("out", [T, D], F32, kind="ExternalOutput").ap()
    X1 = nc.dram_tensor("X1s", [T, D], F32, kind="Internal").ap()
    X2 = nc.dram_tensor("X2s", [T, D], F32, kind="Internal").ap()
    k = K(nc)
    nt = T // 128
    bx = [Buf("x") for _ in range(nt)]
    b1 = [Buf("x1") for _ in range(nt)]
    b2 = [Buf("x2") for _ in range(nt)]
    bo = [Buf("o") for _ in range(nt)]
    phase_ffn(k, "f1", x, X1, bx, b1, w["ffn1_w_gate"], w["ffn1_w_up"], w["ffn1_w_down"], w["ln1_g"], w["ln1_b"], T)
    phase_mix(k, "mx", X1, X2, b1, b2, w["w_in"], w["conv_w"], w["a_log"], w["dt_bias"], w["gdn_norm_w"], w["w_gk"],
              w["b_gk"], w["gla_norm_w"], w["w_out"], w["ln2_g"], w["ln2_b"], T)
    phase_ffn(k, "f2", X2, out, b2, bo, w["ffn2_w_gate"], w["ffn2_w_up"], w["ffn2_w_down"], w["ln3_g"], w["ln3_b"], T)
    return nc


def kernel(**inputs):
    x = np.ascontiguousarray(np.asarray(inputs["x"], dtype=np.float32))
    B, L, _ = x.shape
    wmaps = {}
    for n, shp in W_NAMES:
        wmaps[n] = np.ascontiguousarray(np.asarray(inputs[n], dtype=np.float32)[0].reshape(shp))
    nc = build_full(L)
    in_maps = []
    for c in range(NCORES):
        m = dict(wmaps)
        m["x"] = x[c % B]
        in_maps.append(m)
    res = run_bass_kernel_spmd(nc, in_maps, core_ids=list(range(NCORES)))
    return np.stack([np.asarray(res.results[b]["out"], dtype=np.float32) for b in range(B)], axis=0)
```

```python
import numpy as np
from contextlib import ExitStack
import concourse.bass as bass
import concourse.mybir as mybir
from concourse.bass_utils import run_bass_kernel_spmd

F32 = mybir.dt.float32
BF16 = mybir.dt.bfloat16
ALU = mybir.AluOpType
AF = mybir.ActivationFunctionType

D = 1024
KD = 8
FF = 2816
KF = 22
ALPHA = 2.0 ** 0.25
LN_EPS = 1e-5
RMS_EPS = 1e-6
NCORES = 8


class Sem:
    def __init__(self, h, name):
        self.h = h
        self.v = 0
        self.name = name


class Buf:
    __slots__ = ("name", "w", "r", "excl")

    def __init__(self, name, excl=False):
        self.name = name
        self.w = None
        self.r = {}
        self.excl = excl


class Eng:
    def __init__(self, name, h, is_pe=False):
        self.name = name
        self.h = h
        self.is_pe = is_pe
        self.sem = None
        self.waited = {}


class K:
    def __init__(self, nc):
        self.nc = nc
        self.pe = Eng("pe", nc.tensor, True)
        self.act = Eng("act", nc.scalar)
        self.dve = Eng("dve", nc.vector)
        self.pool = Eng("pool", nc.gpsimd)
        self.sp = Eng("sp", nc.sync)
        self.compute = [self.pe, self.act, self.dve, self.pool]
        self.all = self.compute + [self.sp]
        self.sems = []
        self.nsem = 0
        self.es = None
        self.ses = ExitStack()

    def new_sem(self, name):
        self.nsem += 1
        s = Sem(self.ses.enter_context(self.nc.semaphore(f"{name}_{self.nsem}")), name)
        self.sems.append(s)
        return s

    def begin_phase(self, es, tag):
        self.es = es
        for e in self.compute:
            e.sem = self.new_sem(f"{tag}_{e.name}")

    def _waits(self, eng, R, W):
        need = {}

        def add(tok):
            if tok is None:
                return
            s, v = tok
            if eng.is_pe and s is eng.sem:
                return
            if eng.waited.get(s, 0) >= v:
                return
            if need.get(s, 0) < v:
                need[s] = v

        for b in R:
            add(b.w)
            if b.excl:
                for s, v in b.r.items():
                    if s is not eng.sem:
                        add((s, v))
        for b in W:
            add(b.w)
            for s, v in b.r.items():
                add((s, v))
        for s, v in need.items():
            eng.h.wait_ge(s.h, v)
            eng.waited[s] = v

    def _record(self, tok, R, W):
        s, v = tok
        for b in R:
            if b.r.get(s, 0) < v:
                b.r[s] = v
        for b in W:
            b.w = tok
            b.r = {}

    def op(self, eng, fn, R=(), W=(), sig=True):
        self._waits(eng, R, W)
        ins = fn(eng.h)
        if sig:
            ins.then_inc(eng.sem.h, 1)
            eng.sem.v += 1
            tok = (eng.sem, eng.sem.v)
        else:
            tok = (eng.sem, eng.sem.v + 1)
        self._record(tok, R, W)
        return tok

    def dma(self, eng, pairs, sem, R=(), W=()):
        self._waits(eng, R, W)
        for o, i in pairs:
            eng.h.dma_start(out=o, in_=i).then_inc(sem.h, 16)
            sem.v += 16
        tok = (sem, sem.v)
        self._record(tok, R, W)
        return tok

    def barrier(self):
        for e in self.all:
            for s in self.sems:
                if s.v > 0 and e.waited.get(s, 0) < s.v and not (e.is_pe and s is e.sem):
                    e.h.wait_ge(s.h, s.v)
                    e.waited[s] = s.v


def make_ident(k, es, dt, name):
    nc = k.nc
    t = es.enter_context(nc.sbuf_tensor(name + "_f", [128, 128], F32))
    b = Buf(name)
    k.op(k.pool, lambda e: e.memset(t[:], 0.0), W=[b])
    k.op(k.pool, lambda e: e.affine_select(out=t[:], in_=t[:], pattern=[[-1, 128]], compare_op=ALU.not_equal,
                                           fill=1.0, base=0, channel_multiplier=1), R=[b], W=[b])
    if dt == F32:
        return t, b
    t2 = es.enter_context(nc.sbuf_tensor(name, [128, 128], dt))
    b2 = Buf(name + "c")
    k.op(k.pool, lambda e: e.tensor_copy(t2[:], t[:]), R=[b], W=[b2])
    return t2, b2


def phase_ffn(k, tag, src, dst, srcb, dstb, wg, wu, wd, lng, lnb, T):
    nc = k.nc
    G = 256
    NG = T // G
    with ExitStack() as es:
        k.begin_phase(es, tag)

        def sb(name, shape, dt):
            return es.enter_context(nc.sbuf_tensor(f"{tag}_{name}", shape, dt))

        def ps(name, shape, dt):
            return es.enter_context(nc.psum_tensor(f"{tag}_{name}", shape, dt))

        Wg = sb("Wg", [128, KD, FF], BF16)
        Wu = sb("Wu", [128, KD, FF], BF16)
        Wd = sb("Wd", [128, KF, D], BF16)
        gb = sb("gb", [128, 2, D], F32)
        bWg, bWu, bWd, bgb = Buf("Wg"), Buf("Wu"), Buf("Wd"), Buf("gb")
        ident, bid = make_ident(k, es, BF16, f"{tag}_id")
        xin = [[sb(f"xin{a}{s}", [128, D], F32) for s in range(2)] for a in range(2)]
        bxin = [[Buf("xin") for s in range(2)] for a in range(2)]
        sxin = [[k.new_sem(f"{tag}_xin") for s in range(2)] for a in range(2)]
        xbf = [[sb(f"xbf{a}{s}", [128, D], BF16) for s in range(2)] for a in range(2)]
        bxbf = [[Buf("xbf") for s in range(2)] for a in range(2)]
        xT = [sb(f"xT{a}", [128, KD, G], BF16) for a in range(2)]
        bxT = [Buf("xT") for a in range(2)]
        ssb = [sb(f"ssb{a}", [128, G], F32) for a in range(2)]
        bssb = [Buf("ssb") for a in range(2)]
        hT = [sb(f"hT{a}", [128, G], BF16) for a in range(3)]
        bhT = [Buf("hT") for a in range(3)]
        y = [sb(f"y{a}", [128, D], F32) for a in range(2)]
        by = [Buf("y") for a in range(2)]
        sy = [k.new_sem(f"{tag}_y") for a in range(2)]
        st = [sb(f"st{a}", [128, 16], F32) for a in range(2)]
        bst = [Buf("st") for a in range(2)]
        pT = [ps(f"pT{a}", [128, KD, 128], BF16) for a in range(2)]
        bpT = [Buf("pT", excl=True) for a in range(2)]
        gu = [ps(f"gu{a}", [128, 2, G], F32) for a in range(2)]
        bgu = [Buf("gu", excl=True) for a in range(2)]
        acc = [[ps(f"acc{s}{h}", [128, 512], F32) for h in range(2)] for s in range(2)]
        bacc = [[Buf("acc", excl=True) for h in range(2)] for s in range(2)]

        swg, swu, swd, sgb = (k.new_sem(f"{tag}_w") for _ in range(4))
        wgv = wg.rearrange("(k p) f -> p k f", p=128)
        wuv = wu.rearrange("(k p) f -> p k f", p=128)
        wdv = wd.rearrange("(c p) n -> p c n", p=128)
        k.dma(k.sp, [(gb[:, 0, :], lng.partition_broadcast(128)), (gb[:, 1, :], lnb.partition_broadcast(128))],
              sgb, W=[bgb])
        k.dma(k.pool, [(Wg[:, kk, :], wgv[:, kk, :]) for kk in range(KD)], swg, W=[bWg])
        k.dma(k.pool, [(Wu[:, kk, :], wuv[:, kk, :]) for kk in range(KD)], swu, W=[bWu])
        k.dma(k.pool, [(Wd[:, c:c + 2, :], wdv[:, c:c + 2, :]) for c in range(0, KF, 2)], swd, W=[bWd])

        def load_x(g):
            a = g % 2
            for s in range(2):
                t = g * 2 + s
                k.dma(k.sp, [(xin[a][s][:], src[t * 128:(t + 1) * 128, :])], sxin[a][s], R=[srcb[t]], W=[bxin[a][s]])

        def prep_x(g):
            a = g % 2
            for s in range(2):
                k.op(k.pool, lambda e: e.tensor_copy(xbf[a][s][:], xin[a][s][:]), R=[bxin[a][s]], W=[bxbf[a][s]])
                for kk in range(KD):
                    k.op(k.pe, lambda e: e.transpose(pT[s][:, kk, :], xbf[a][s][:, kk * 128:(kk + 1) * 128], ident[:]),
                         R=[bxbf[a][s], bid], W=[bpT[s]], sig=(kk == KD - 1))
                k.op(k.act, lambda e: e.copy(xT[a][:, :, s * 128:(s + 1) * 128], pT[s][:]), R=[bpT[s]], W=[bxT[a]])

        def gu_mm(g, c):
            a = g % 2
            sl = c % 2
            for j, (Wt, bW) in enumerate(((Wg, bWg), (Wu, bWu))):
                for kk in range(KD):
                    k.op(k.pe, lambda e: e.matmul(gu[sl][:, j, :], Wt[:, kk, c * 128:(c + 1) * 128], xT[a][:, kk, :],
                                                  start=(kk == 0), stop=(kk == KD - 1)),
                         R=[bW, bxT[a]], W=[bgu[sl]], sig=(j == 1 and kk == KD - 1))

        def act_h(g, c):
            sl = c % 2
            h3 = c % 3
            k.op(k.act, lambda e: e.activation(out=ssb[sl][:], in_=gu[sl][:, 0, :], func=AF.Silu),
                 R=[bgu[sl]], W=[bssb[sl]])
            k.op(k.dve, lambda e: e.scalar_tensor_tensor(out=hT[h3][:], in0=ssb[sl][:], scalar=0.5, in1=gu[sl][:, 1, :],
                                                         op0=ALU.mult, op1=ALU.mult),
                 R=[bssb[sl], bgu[sl]], W=[bhT[h3]])

        def down_mm(g, c):
            h3 = c % 3
            for s in range(2):
                for h in range(2):
                    k.op(k.pe, lambda e: e.matmul(acc[s][h][:], hT[h3][:, s * 128:(s + 1) * 128],
                                                  Wd[:, c, h * 512:(h + 1) * 512], start=(c == 0), stop=(c == KF - 1)),
                         R=[bhT[h3], bWd], W=[bacc[s][h]], sig=(s == 1 and h == 1))

        def epilogue(g):
            a = g % 2
            for s in range(2):
                t = g * 2 + s
                yy, byy, stt, bstt = y[s], by[s], st[s], bst[s]
                for h in range(2):
                    k.op(k.dve, lambda e: e.scalar_tensor_tensor(out=yy[:, h * 512:(h + 1) * 512],
                                                                 in0=xin[a][s][:, h * 512:(h + 1) * 512], scalar=ALPHA,
                                                                 in1=acc[s][h][:], op0=ALU.mult, op1=ALU.add),
                         R=[bxin[a][s], bacc[s][h]], W=[byy])
                layer_norm(k, yy, byy, stt, bstt, gb, bgb)
                k.dma(k.sp, [(dst[t * 128:(t + 1) * 128, :], yy[:])], sy[s], R=[byy], W=[dstb[t]])

        load_x(0)
        if NG > 1:
            load_x(1)
        prep_x(0)
        for g in range(NG):
            gu_mm(g, 0)
            for c in range(KF):
                if c + 1 < KF:
                    gu_mm(g, c + 1)
                elif g + 1 < NG:
                    prep_x(g + 1)
                act_h(g, c)
                down_mm(g, c)
            epilogue(g)
            if g + 2 < NG:
                load_x(g + 2)
        k.barrier()


def layer_norm(k, yy, byy, stt, bstt, gb, bgb):
    for h in range(2):
        k.op(k.dve, lambda e: e.bn_stats(out=stt[:, h * 6:(h + 1) * 6], in_=yy[:, h * 512:(h + 1) * 512]),
             R=[byy], W=[bstt])
    k.op(k.dve, lambda e: e.bn_aggr(out=stt[:, 12:14], in_=stt[:, 0:12]), R=[bstt], W=[bstt])
    k.op(k.dve, lambda e: e.tensor_scalar_add(stt[:, 13:14], stt[:, 13:14], LN_EPS), R=[bstt], W=[bstt])
    k.op(k.act, lambda e: e.activation(out=stt[:, 14:15], in_=stt[:, 13:14], func=AF.Sqrt), R=[bstt], W=[bstt])
    k.op(k.dve, lambda e: e.reciprocal(out=stt[:, 14:15], in_=stt[:, 14:15]), R=[bstt], W=[bstt])
    k.op(k.dve, lambda e: e.scalar_tensor_tensor(out=stt[:, 15:16], in0=stt[:, 12:13], scalar=-1.0, in1=stt[:, 14:15],
                                                 op0=ALU.mult, op1=ALU.mult), R=[bstt], W=[bstt])
    k.op(k.act, lambda e: e.activation(out=yy[:], in_=yy[:], func=AF.Identity, scale=stt[:, 14:15], bias=stt[:, 15:16]),
         R=[byy, bstt], W=[byy])
    k.op(k.pool, lambda e: e.tensor_tensor(out=yy[:], in0=yy[:], in1=gb[:, 0, :], op=ALU.mult), R=[byy, bgb], W=[byy])
    k.op(k.pool, lambda e: e.tensor_tensor(out=yy[:], in0=yy[:], in1=gb[:, 1, :], op=ALU.add), R=[byy, bgb], W=[byy])


def build_ffn_only(T):
    nc = bass.Bass("TRN2", target_bir_lowering=False)
    x = nc.dram_tensor("x", [T, D], F32, kind="ExternalInput").ap()
    wg = nc.dram_tensor("ffn1_w_gate", [D, FF], F32, kind="ExternalInput").ap()
    wu = nc.dram_tensor("ffn1_w_up", [D, FF], F32, kind="ExternalInput").ap()
    wd = nc.dram_tensor("ffn1_w_down", [FF, D], F32, kind="ExternalInput").ap()
    lg = nc.dram_tensor("ln1_g", [1, D], F32, kind="ExternalInput").ap()
    lb = nc.dram_tensor("ln1_b", [1, D], F32, kind="ExternalInput").ap()
    out = nc.dram_tensor("out", [T, D], F32, kind="ExternalOutput").ap()
    k = K(nc)
    srcb = [Buf("x") for _ in range(T // 128)]
    dstb = [Buf("o") for _ in range(T // 128)]
    phase_ffn(k, "f1", x, out, srcb, dstb, wg, wu, wd, lg, lb, T)
    return nc


GT = 256
NCH = GT // 64
C_Z = 1536
C_BETA = 2048
C_QB = 2056
C_KB = 2312
C_VB = 2568
C_GB = 3080
C_LR = 3592
PROJ = 3608
BIG = 1.0e30


class _Stop(Exception):
    pass


def phase_mix(k, tag, src, dst, srcb, dstb, w_in, conv_w, a_log, dt_bias, gdn_nw, w_gk, b_gk, gla_nw, w_out,
              lng, lnb, T, dbg=None, stop=0):
    try:
        _phase_mix(k, tag, src, dst, srcb, dstb, w_in, conv_w, a_log, dt_bias, gdn_nw, w_gk, b_gk, gla_nw, w_out,
                   lng, lnb, T, dbg, stop)
    except _Stop:
        pass
    k.barrier()


def _phase_mix(k, tag, src, dst, srcb, dstb, w_in, conv_w, a_log, dt_bias, gdn_nw, w_gk, b_gk, gla_nw, w_out,
               lng, lnb, T, dbg, stop):
    nc = k.nc
    NG = T // GT

    def chk(n):
        if stop == n:
            raise _Stop()

    with ExitStack() as es:
        k.begin_phase(es, tag)
        pe, act, dve, pool, sp = k.pe, k.act, k.dve, k.pool, k.sp

        def sb(name, shape, dt=F32):
            return es.enter_context(nc.sbuf_tensor(f"{tag}_{name}", shape, dt)), Buf(name)

        banks = [es.enter_context(nc.psum_tensor(f"{tag}_bk{i}", [128, 512], F32)) for i in range(7)]
        pTb = es.enter_context(nc.psum_tensor(f"{tag}_pT", [128, KD, 128], BF16))
        bpT = Buf("pT", excl=True)
        bbank = [Buf(f"bk{i}", excl=True) for i in range(7)]
        cnt = {"r": 0}

        def P2():
            i = cnt["r"] % 7
            cnt["r"] += 1
            return banks[i], bbank[i]

        def P1():
            b, bb = P2()
            return b, 0, bb

        Win, bWin = sb("Win", [128, KD, PROJ], BF16)
        Wo, bWo = sb("Wo", [128, KD, D], BF16)
        gbt, bgb = sb("gb", [128, 2, D])
        cw, bcw = sb("cw", [128, 12, 4])
        nw, bnw = sb("nw", [128, 2])
        wgk, bwgk = sb("wgk", [16, 256])
        bgk, bbgk = sb("bgk", [1, 256])
        hc, bhc = sb("hc", [64, 2, 4])
        hcc, bhcc = sb("hcc", [64, 2, NCH, 4])
        identb, bidb = make_ident(k, es, BF16, f"{tag}_idb")
        identf, bidf = make_ident(k, es, F32, f"{tag}_idf")
        cU, bcU = sb("cU", [64, 4, 64])
        cUn, bcUn = sb("cUn", [64, 64])
        cSLn, bcSLn = sb("cSLn", [64, 64])
        cMA, bcMA = sb("cMA", [64, 4, 64])
        cMQ, bcMQ = sb("cMQ", [64, 4, 64])
        cR0, bcR0 = sb("cR0", [64, 4, 64])
        ones, bones = sb("ones", [128, 128])
        nones, bnones = sb("nones", [64, 64])
        CONST = [bcU, bcUn, bcSLn, bcMA, bcMQ, bcR0, bones, bnones, bidf, bidb]

        sW = [k.new_sem(f"{tag}_w") for _ in range(3)]
        winv = w_in.rearrange("(k p) f -> p k f", p=128)
        wov = w_out.rearrange("(k p) f -> p k f", p=128)
        k.dma(pool, [(Win[:, kk, :], winv[:, kk, :]) for kk in range(KD)], sW[0], W=[bWin])
        k.dma(pool, [(Wo[:, kk, :], wov[:, kk, :]) for kk in range(KD)], sW[1], W=[bWo])
        with nc.allow_non_contiguous_dma(reason="tiny constant loads"):
            k.dma(sp, [(gbt[:, 0, :], lng.partition_broadcast(128)), (gbt[:, 1, :], lnb.partition_broadcast(128)),
                       (nw[:, 0:1], gdn_nw.rearrange("o (p u) -> (o p) u", u=1)),
                       (nw[:, 1:2], gla_nw.rearrange("o (p u) -> (o p) u", u=1)),
                       (wgk[:], w_gk), (bgk[:], b_gk),
                       (hc[:, 0, :], a_log.partition_broadcast(64)), (hc[:, 1, :], dt_bias.partition_broadcast(64))],
                  sW[2], W=[bgb, bnw, bwgk, bbgk, bhc])

        def psel(t, b, base_val, pattern, cm, cmp, fill):
            k.op(pool, lambda e: e.memset(t, base_val), W=[b])
            k.op(pool, lambda e: e.affine_select(out=t, in_=t, pattern=pattern, compare_op=cmp, fill=fill, base=0,
                                                 channel_multiplier=cm), R=[b], W=[b])

        psel(cU[:], bcU, 1.0, [[0, 4], [1, 64]], -1, ALU.is_ge, 0.0)
        psel(cUn[:], bcUn, -1.0 / 16, [[1, 64]], -1, ALU.is_ge, 0.0)
        psel(cSLn[:], bcSLn, -1.0 / 16, [[-1, 64]], 1, ALU.is_gt, 0.0)
        psel(cMA[:], bcMA, 0.0, [[0, 4], [-1, 64]], 1, ALU.is_gt, -BIG)
        psel(cMQ[:], bcMQ, 0.0, [[0, 4], [1, 64]], -1, ALU.is_ge, BIG)
        psel(cR0[:], bcR0, 0.0, [[0, 4], [-1, 64]], 1, ALU.not_equal, 1.0)
        k.op(pool, lambda e: e.memset(ones[:], 1.0), W=[bones])
        k.op(pool, lambda e: e.memset(nones[:], -1.0), W=[bnones])
        k.op(act, lambda e: e.activation(out=hc[:, 0, :], in_=hc[:, 0, :], func=AF.Exp), R=[bhc], W=[bhc])
        for c in range(NCH):
            k.op(dve, lambda e: e.tensor_scalar_mul(hcc[:, 0, c, :], hc[:, 0, :], -1.0), R=[bhc], W=[bhcc])
            k.op(dve, lambda e: e.tensor_copy(hcc[:, 1, c, :], hc[:, 1, :]), R=[bhc], W=[bhcc])

        xin = [sb(f"xin{s}", [128, D]) for s in range(2)]
        sxin = [k.new_sem(f"{tag}_xin") for s in range(2)]
        xbf = [sb(f"xbf{s}", [128, D], BF16) for s in range(2)]
        xT, bxT = sb("xT", [128, KD, GT], BF16)
        cb = [sb(f"cb{t}", [128, GT + 3]) for t in range(12)]
        cv, bcv = sb("cv", [128, GT])
        cs = [sb(f"cs{t}", [128, GT]) for t in range(12)]
        sq, bsq = cv, bcv
        rn, brn = sb("rn", [128, GT])
        qkb = [sb(f"qkb{t}", [128, GT], BF16) for t in range(8)]
        sz = [sb(f"sz{t}", [128, GT], BF16) for t in range(8)]
        qbT = [sb(f"qbT{p}", [64, GT]) for p in range(4)]
        kbT = [sb(f"kbT{p}", [64, GT]) for p in range(4)]
        lrT, blrT = sb("lrT", [16, GT])
        kbt, bkbt = sb("kbt", [64, NCH, 256])
        vbt, bvbt = sb("vbt", [64, NCH, 512])
        cwr, bcwr = vbt[0:4, 0:3, :].rearrange("p a b -> p (a b)"), bvbt
        bd, bbd = sb("bd", [64, NCH, 8])
        beta, bbeta = sb("beta", [64, NCH, 4])
        nbeta, bnbeta = sb("nbeta", [64, NCH, 4])
        gtok, bgtok = sb("gtok", [64, NCH, 4])
        tmp8, btmp8 = sb("tmp8", [64, NCH, 4])
        oT, boT = sb("oT", [128, 8, GT])
        mixT, bmixT = sb("mixT", [128, 8, GT], BF16)
        sy = [k.new_sem(f"{tag}_y") for s in range(2)]
        st = [sb(f"st{s}", [128, 16]) for s in range(2)]
        Sg, bSg = sb("Sg", [128, 4, 128])
        Sl, bSl = sb("Sl", [64, 4, 128])
        gU, bgU = sb("gU", [64, 4, 64])
        tA, btA = sb("tA", [64, 4, 64])
        tQ, btQ = sb("tQ", [64, 4, 64])
        EGb, bEGb = sb("EGb", [128, 4, 64])
        sc, bsc = sb("sc", [64, 12])
        M = [sb(f"M{i}", [64, 4, 64]) for i in range(2)]
        N = [sb(f"N{i}", [64, 4, 64]) for i in range(2)]
        Rm = [sb(f"R{i}", [64, 4, 64]) for i in range(2)]
        qkd, bqkd = sb("qkd", [64, 4, 64])
        bk_, bbk = sb("bk", [64, 4, 128])
        kd_, bkd = sb("kd", [64, 4, 128])
        bv_, bbv = sb("bv", [64, 4, 128])
        wT, bwT = sb("wT", [128, 4, 64])
        u_, bu = sb("u", [64, 4, 128])
        qd, bqd = sb("qd", [128, 4, 64])
        vn, bvn = sb("vn", [64, 4, 128])
        lt, blt = sb("lt", [64, 256])
        kdec, bkdec = sb("kdec", [64, 256])
        ebT, bebT = sb("ebT", [64, 4, 64])
        enbT, benbT = sb("enbT", [64, 4, 64])
        qt, bqt = sb("qt", [64, 4, 64])
        kt, bkt = sb("kt", [64, 4, 64])
        att, batt = sb("att", [64, 4, 64])

        k.dma(sp, [(cwr, conv_w)], k.new_sem(f"{tag}_cw"), W=[bcwr])
        for t in range(12):
            bank, off, pb = P1()
            k.op(pe, lambda e: e.transpose(bank[:, off:off + 4], cwr[:, t * 128:(t + 1) * 128], identf[0:4, 0:4]),
                 R=[bcwr, bidf], W=[pb])
            k.op(act, lambda e: e.copy(cw[:, t, :], bank[:, off:off + 4]), R=[pb], W=[bcw])
        k.op(pool, lambda e: e.memset(Sg[:], 0.0), W=[bSg])
        k.op(pool, lambda e: e.memset(Sl[:], 0.0), W=[bSl])
        for t in range(12):
            k.op(pool, lambda e: e.memset(cb[t][0][:, 0:3], 0.0), W=[cb[t][1]])

        def v3(bank, off, p, a, b_):
            return bank[0:p, off:off + a * b_].rearrange("p (a b) -> p a b", a=a)

        sdbg = k.new_sem(f"{tag}_dbg")
        chk(1)
        for g in range(NG):
            for s in range(2):
                t = g * 2 + s
                k.dma(sp, [(xin[s][0][:], src[t * 128:(t + 1) * 128, :])], sxin[s], R=[srcb[t]], W=[xin[s][1]])
                k.op(pool, lambda e: e.tensor_copy(xbf[s][0][:], xin[s][0][:]), R=[xin[s][1]], W=[xbf[s][1]])
                for kk in range(KD):
                    k.op(pe, lambda e: e.transpose(pTb[:, kk, :], xbf[s][0][:, kk * 128:(kk + 1) * 128], identb[:]),
                         R=[xbf[s][1], bidb], W=[bpT], sig=(kk == KD - 1))
                k.op(act, lambda e: e.copy(xT[:, :, s * 128:(s + 1) * 128], pTb[:]), R=[bpT], W=[bxT])

            def proj_fm(col, m):
                bank, off, pb = P1()
                for kk in range(KD):
                    k.op(pe, lambda e: e.matmul(bank[0:m, off:off + GT], Win[:, kk, col:col + m], xT[:, kk, :],
                                                start=(kk == 0), stop=(kk == KD - 1)),
                         R=[bWin, bxT], W=[pb], sig=(kk == KD - 1))
                return bank[0:m, off:off + GT], pb

            for t in range(12):
                ps_, pb = proj_fm(t * 128, 128)
                cbt, bcb = cb[t]
                k.op(act, lambda e: e.copy(cbt[:, 3:3 + GT], ps_), R=[pb], W=[bcb])
                k.op(pool, lambda e: e.tensor_scalar_mul(cv[:], cbt[:, 0:GT], cw[:, t, 0:1]), R=[bcb, bcw], W=[bcv])
                for j in range(1, 4):
                    k.op(dve, lambda e: e.scalar_tensor_tensor(out=cv[:], in0=cbt[:, j:j + GT], scalar=cw[:, t, j:j + 1],
                                                                in1=cv[:], op0=ALU.mult, op1=ALU.add),
                         R=[bcb, bcw, bcv], W=[bcv])
                k.op(pool, lambda e: e.tensor_copy(cbt[:, 0:3], cbt[:, GT:GT + 3]), R=[bcb], W=[bcb])
                k.op(act, lambda e: e.activation(out=cs[t][0][:], in_=cv[:], func=AF.Silu), R=[bcv], W=[cs[t][1]])
            for t in range(8):
                col = (C_Z + t * 128) if t < 4 else (C_GB + (t - 4) * 128)
                ps_, pb = proj_fm(col, 128)
                k.op(act, lambda e: e.activation(out=sz[t][0][:], in_=ps_, func=AF.Silu), R=[pb], W=[sz[t][1]])
            for p in range(4):
                ps_, pb = proj_fm(C_QB + p * 64, 64)
                k.op(act, lambda e: e.copy(qbT[p][0][:], ps_), R=[pb], W=[qbT[p][1]])
                ps_, pb = proj_fm(C_KB + p * 64, 64)
                k.op(act, lambda e: e.copy(kbT[p][0][:], ps_), R=[pb], W=[kbT[p][1]])
            ps_, pb = proj_fm(C_LR, 16)
            k.op(act, lambda e: e.copy(lrT[:], ps_), R=[pb], W=[blrT])
            for c in range(NCH):
                tok = slice(c * 64, (c + 1) * 64)
                bank, off, pb = P1()
                for kk in range(KD):
                    k.op(pe, lambda e: e.matmul(bank[0:64, off:off + 8], xT[:, kk, tok], Win[:, kk, C_BETA:C_BETA + 8],
                                                start=(kk == 0), stop=(kk == KD - 1)),
                         R=[bWin, bxT], W=[pb], sig=(kk == KD - 1))
                k.op(act, lambda e: e.copy(bd[:, c, :], bank[0:64, off:off + 8]), R=[pb], W=[bbd])
                bank, off, pb = P1()
                for kk in range(KD):
                    k.op(pe, lambda e: e.matmul(bank[0:64, off:off + 256], xT[:, kk, tok], Win[:, kk, C_KB:C_KB + 256],
                                                start=(kk == 0), stop=(kk == KD - 1)),
                         R=[bWin, bxT], W=[pb], sig=(kk == KD - 1))
                k.op(act, lambda e: e.copy(kbt[:, c, :], bank[0:64, off:off + 256]), R=[pb], W=[bkbt])
                bank2, pb2 = P2()
                for kk in range(KD):
                    k.op(pe, lambda e: e.matmul(bank2[0:64, :], xT[:, kk, tok], Win[:, kk, C_VB:C_VB + 512],
                                                start=(kk == 0), stop=(kk == KD - 1)),
                         R=[bWin, bxT], W=[pb2], sig=(kk == KD - 1))
                k.op(act, lambda e: e.copy(vbt[:, c, :], bank2[0:64, :]), R=[pb2], W=[bvbt])
            chk(2)
            for t in range(8):
                cst, bcs = cs[t]
                k.op(pool, lambda e: e.tensor_tensor(out=sq[:], in0=cst[:], in1=cst[:], op=ALU.mult), R=[bcs], W=[bsq])
                bank, off, pb = P1()
                k.op(pe, lambda e: e.matmul(bank[:, off:off + GT], ones[:], sq[:], start=True, stop=True),
                     R=[bones, bsq], W=[pb])
                k.op(act, lambda e: e.activation(out=rn[:], in_=bank[:, off:off + GT], func=AF.Sqrt, bias=RMS_EPS),
                     R=[pb], W=[brn])
                k.op(dve, lambda e: e.reciprocal(out=rn[:], in_=rn[:]), R=[brn], W=[brn])
                scl = (128.0 ** -0.5) if t < 4 else 1.0
                k.op(dve, lambda e: e.scalar_tensor_tensor(out=cst[:], in0=cst[:], scalar=scl, in1=rn[:], op0=ALU.mult,
                                                           op1=ALU.mult), R=[bcs, brn], W=[bcs])
                k.op(pool, lambda e: e.tensor_copy(qkb[t][0][:], cst[:]), R=[bcs], W=[qkb[t][1]])
            k.op(act, lambda e: e.activation(out=beta[:], in_=bd[:, :, 0:4], func=AF.Sigmoid), R=[bbd], W=[bbeta])
            k.op(dve, lambda e: e.tensor_scalar_mul(nbeta[:], beta[:], -1.0), R=[bbeta], W=[bnbeta])
            k.op(dve, lambda e: e.tensor_tensor(out=tmp8[:], in0=bd[:, :, 4:8], in1=hcc[:, 1, :, :], op=ALU.add),
                 R=[bbd, bhcc], W=[btmp8])
            k.op(act, lambda e: e.activation(out=tmp8[:], in_=tmp8[:], func=AF.Exp), R=[btmp8], W=[btmp8])
            k.op(act, lambda e: e.activation(out=tmp8[:], in_=tmp8[:], func=AF.Ln, bias=1.0), R=[btmp8], W=[btmp8])
            k.op(dve, lambda e: e.tensor_tensor(out=gtok[:], in0=tmp8[:], in1=hcc[:, 0, :, :], op=ALU.mult),
                 R=[btmp8, bhcc], W=[bgtok])

            chk(3)
            for c in range(NCH):
                tok = slice(c * 64, (c + 1) * 64)
                def gla_a():
                    bankZ, offZ, pZ = P1()
                    k.op(pe, lambda e: e.matmul(bankZ[0:64, offZ:offZ + 256], lrT[:, tok], wgk[:], start=True, stop=False),
                         R=[blrT, bwgk], W=[pZ], sig=False)
                    k.op(pe, lambda e: e.matmul(bankZ[0:64, offZ:offZ + 256], ones[0:1, 0:64], bgk[:], start=False, stop=True),
                         R=[bones, bbgk], W=[pZ])
                    k.op(act, lambda e: e.activation(out=lt[:], in_=bankZ[0:64, offZ:offZ + 256], func=AF.Exp, scale=-1.0),
                         R=[pZ], W=[blt])
                    k.op(act, lambda e: e.activation(out=lt[:], in_=lt[:], func=AF.Ln, bias=1.0), R=[blt], W=[blt])

                def gla_b():
                    bankB, offB, pB = P1()
                    bps = v3(bankB, offB, 64, 4, 64)
                    for h in range(4):
                        k.op(pe, lambda e: e.matmul(bps[:, h, :], lt[:, h * 64:(h + 1) * 64], cUn[:], start=True, stop=True),
                             R=[blt, bcUn], W=[pB], sig=(h == 3))
                    bankE, offE, pE = P1()
                    k.op(pe, lambda e: e.matmul(bankE[0:64, offE:offE + 256], cSLn[:], lt[:], start=True, stop=True),
                         R=[blt, bcSLn], W=[pE])
                    k.op(act, lambda e: e.activation(out=ebT[:], in_=bps, func=AF.Exp), R=[pB], W=[bebT])
                    k.op(act, lambda e: e.activation(out=enbT[:], in_=bps, func=AF.Exp, scale=-1.0), R=[pB], W=[benbT])
                    k.op(act, lambda e: e.activation(out=kdec[:], in_=bankE[0:64, offE:offE + 256], func=AF.Exp), R=[pE], W=[bkdec])
                    k.op(pool, lambda e: e.tensor_tensor(out=kdec[:], in0=kdec[:], in1=kbt[:, c, :], op=ALU.mult),
                         R=[bkdec, bkbt], W=[bkdec])
                    for h in range(4):
                        k.op(dve, lambda e: e.scalar_tensor_tensor(out=qt[:, h, :], in0=qbT[h][0][:, tok], scalar=0.125,
                                                                   in1=ebT[:, h, :], op0=ALU.mult, op1=ALU.mult),
                             R=[qbT[h][1], bebT], W=[bqt])
                        k.op(pool, lambda e: e.tensor_tensor(out=kt[:, h, :], in0=kbT[h][0][:, tok], in1=enbT[:, h, :], op=ALU.mult),
                             R=[kbT[h][1], benbT], W=[bkt])

                def gla_c():
                    bankA, offA, pA = P1()
                    aps = v3(bankA, offA, 64, 4, 64)
                    for h in range(4):
                        k.op(pe, lambda e: e.matmul(aps[:, h, :], kt[:, h, :], qt[:, h, :], start=True, stop=True),
                             R=[bkt, bqt], W=[pA], sig=(h == 3))
                    k.op(dve, lambda e: e.tensor_tensor(out=att[:], in0=aps, in1=cU[:], op=ALU.mult), R=[pA, bcU], W=[batt])
                    bankO2, offO2, pO2 = P1()
                    ops2 = v3(bankO2, offO2, 128, 4, 64)
                    for h in range(4):
                        k.op(pe, lambda e: e.matmul(ops2[:, h, :], Sl[:, h, :], qt[:, h, :], start=True, stop=False),
                             R=[bSl, bqt], W=[pO2], sig=False)
                        k.op(pe, lambda e: e.matmul(ops2[:, h, :], vbt[:, c, h * 128:(h + 1) * 128], att[:, h, :], start=False, stop=True),
                             R=[bvbt, batt], W=[pO2], sig=(h == 3))
                    k.op(act, lambda e: e.copy(oT[:, 4:8, tok], ops2), R=[pO2], W=[boT])
                    bankS2, pS2 = P2()
                    ds2 = v3(bankS2, 0, 64, 4, 128)
                    for h in range(4):
                        k.op(pe, lambda e: e.matmul(ds2[:, h, :], kdec[:, h * 64:(h + 1) * 64], vbt[:, c, h * 128:(h + 1) * 128],
                                                    start=True, stop=True), R=[bkdec, bvbt], W=[pS2], sig=(h == 3))
                    for h in range(4):
                        k.op(dve, lambda e: e.scalar_tensor_tensor(out=Sl[:, h, :], in0=Sl[:, h, :], scalar=ebT[:, h, 63:64],
                                                                   in1=ds2[:, h, :], op0=ALU.mult, op1=ALU.add),
                             R=[bSl, bebT, pS2], W=[bSl])


                gla_a()
                for h in range(4):
                    k.op(dve, lambda e: e.tensor_scalar_mul(gU[:, h, :], cU[:, h, :], gtok[:, c, h:h + 1]),
                         R=[bcU, bgtok], W=[bgU])
                bank, off, pG = P1()
                k.op(pe, lambda e: e.matmul(bank[0:64, off:off + 4], cU[:, 0, :], gtok[:, c, :], start=True, stop=True),
                     R=[bcU, bgtok], W=[pG])
                k.op(act, lambda e: e.activation(out=sc[:, 0:4], in_=bank[0:64, off:off + 4], func=AF.Exp), R=[pG], W=[bsc])
                k.op(dve, lambda e: e.tensor_tensor(out=sc[:, 4:8], in0=sc[:, 0:4], in1=beta[:, c, :], op=ALU.mult),
                     R=[bsc, bbeta], W=[bsc])
                bankD, offD, pD = P1()
                Dps = v3(bankD, offD, 64, 4, 64)
                for h in range(4):
                    k.op(pe, lambda e: e.matmul(Dps[:, h, :], nones[:], gU[:, h, :], start=True, stop=False),
                         R=[bnones, bgU], W=[pD], sig=False)
                    k.op(pe, lambda e: e.matmul(Dps[:, h, :], gU[:, h, :], ones[0:64, 0:64], start=False, stop=True),
                         R=[bones, bgU], W=[pD], sig=(h == 3))
                bankG, offG, pGb = P1()
                k.op(pe, lambda e: e.matmul(bankG[:, offG:offG + 256], ones[0:64, :], gU[:].rearrange("p a b -> p (a b)"),
                                            start=True, stop=True), R=[bones, bgU], W=[pGb])
                k.op(dve, lambda e: e.tensor_tensor(out=tA[:], in0=Dps, in1=cMA[:], op=ALU.add), R=[pD, bcMA], W=[btA])
                k.op(dve, lambda e: e.tensor_tensor(out=tQ[:], in0=Dps, in1=cMQ[:], op=ALU.add), R=[pD, bcMQ], W=[btQ])
                k.op(act, lambda e: e.activation(out=sc[:, 8:12], in_=Dps[:, :, 63], func=AF.Exp, scale=-1.0),
                     R=[pD], W=[bsc])
                k.op(act, lambda e: e.activation(out=tA[:], in_=tA[:], func=AF.Exp), R=[btA], W=[btA])
                k.op(act, lambda e: e.activation(out=tQ[:], in_=tQ[:], func=AF.Exp, scale=-1.0), R=[btQ], W=[btQ])
                k.op(act, lambda e: e.activation(out=EGb[:].rearrange("p a b -> p (a b)"), in_=bankG[:, offG:offG + 256],
                                                 func=AF.Exp), R=[pGb], W=[bEGb])
                chk(31)
                bankK, offK, pK = P1()
                kkps = v3(bankK, offK, 64, 4, 64)
                bankQ, offQ, pQ = P1()
                qkps = v3(bankQ, offQ, 64, 4, 64)
                for h in range(4):
                    k.op(pe, lambda e: e.matmul(kkps[:, h, :], qkb[4 + h][0][:, tok], qkb[4 + h][0][:, tok], start=True,
                                                stop=True), R=[qkb[4 + h][1]], W=[pK], sig=(h == 3))
                for h in range(4):
                    k.op(pe, lambda e: e.matmul(qkps[:, h, :], qkb[4 + h][0][:, tok], qkb[h][0][:, tok], start=True,
                                                stop=True), R=[qkb[4 + h][1], qkb[h][1]], W=[pQ], sig=(h == 3))
                M0, bM0 = M[0]
                for h in range(4):
                    k.op(dve, lambda e: e.scalar_tensor_tensor(out=M0[:, h, :], in0=kkps[:, h, :], scalar=nbeta[:, c, h:h + 1],
                                                               in1=tA[:, h, :], op0=ALU.mult, op1=ALU.mult),
                         R=[pK, bnbeta, btA], W=[bM0])
                k.op(dve, lambda e: e.tensor_tensor(out=qkd[:], in0=qkps, in1=tQ[:], op=ALU.mult), R=[pQ, btQ], W=[bqkd])
                gla_b()
                chk(32)
                bankN, offN, pN = P1()
                Nps = v3(bankN, offN, 64, 4, 64)
                for h in range(4):
                    k.op(pe, lambda e: e.matmul(Nps[:, h, :], M0[:, h, :], identf[0:64, 0:64], start=True, stop=True),
                         R=[bM0, bidf], W=[pN], sig=(h == 3))
                N0, bN0 = N[0]
                k.op(act, lambda e: e.copy(N0[:], Nps), R=[pN], W=[bN0])
                k.op(dve, lambda e: e.tensor_tensor(out=Rm[0][0][:], in0=Nps, in1=cR0[:], op=ALU.add), R=[pN, bcR0],
                     W=[Rm[0][1]])
                chk(33)
                for i in range(5):
                    Mi, bMi = M[i % 2]
                    Mn, bMn = M[(i + 1) % 2]
                    Ni, bNi = N[i % 2]
                    Nn, bNn = N[(i + 1) % 2]
                    Ri, bRi = Rm[i % 2]
                    Rn_, bRn = Rm[(i + 1) % 2]
                    bankM, offM, pM = P1()
                    Mps = v3(bankM, offM, 64, 4, 64)
                    for h in range(4):
                        k.op(pe, lambda e: e.matmul(Mps[:, h, :], Ni[:, h, :], Mi[:, h, :], start=True, stop=True),
                             R=[bNi, bMi], W=[pM], sig=(h == 3))
                    if i < 4:
                        bankN2, offN2, pN2 = P1()
                        Nps2 = v3(bankN2, offN2, 64, 4, 64)
                        for h in range(4):
                            k.op(pe, lambda e: e.matmul(Nps2[:, h, :], Mi[:, h, :], Ni[:, h, :], start=True, stop=True),
                                 R=[bNi, bMi], W=[pN2], sig=(h == 3))
                    k.op(act, lambda e: e.copy(Mn[:], Mps), R=[pM], W=[bMn])
                    if i < 4:
                        k.op(dve, lambda e: e.tensor_copy(Nn[:], Nps2), R=[pN2], W=[bNn])
                    bankR, offR, pR = P1()
                    Rps = v3(bankR, offR, 64, 4, 64)
                    for h in range(4):
                        k.op(pe, lambda e: e.matmul(Rps[:, h, :], Mn[:, h, :], Ri[:, h, :], start=True, stop=True),
                             R=[bMn, bRi], W=[pR], sig=(h == 3))
                    k.op(dve, lambda e: e.tensor_tensor(out=Rn_[:], in0=Rps, in1=Ri[:], op=ALU.add), R=[pR, bRi], W=[bRn])
                    if i == 1:
                        gla_c()
                Rf, bRf = Rm[1]
                chk(34)
                bankKt, pKt = P2()
                ktps = v3(bankKt, 0, 64, 4, 128)
                bankVt, pVt = P2()
                vtps = v3(bankVt, 0, 64, 4, 128)
                for h in range(4):
                    k.op(pe, lambda e: e.matmul(ktps[:, h, :], cs[4 + h][0][:, tok], identf[:], start=True, stop=True),
                         R=[cs[4 + h][1], bidf], W=[pKt], sig=(h == 3))
                for h in range(4):
                    k.op(pe, lambda e: e.matmul(vtps[:, h, :], cs[8 + h][0][:, tok], identf[:], start=True, stop=True),
                         R=[cs[8 + h][1], bidf], W=[pVt], sig=(h == 3))
                for h in range(4):
                    k.op(act, lambda e: e.activation(out=bk_[:, h, :], in_=ktps[:, h, :], func=AF.Identity, scale=sc[:, 4 + h:5 + h]),
                         R=[pKt, bsc], W=[bbk])
                    k.op(act, lambda e: e.activation(out=kd_[:, h, :], in_=ktps[:, h, :], func=AF.Identity, scale=sc[:, 8 + h:9 + h]),
                         R=[pKt, bsc], W=[bkd])
                    k.op(act, lambda e: e.activation(out=bv_[:, h, :], in_=vtps[:, h, :], func=AF.Identity, scale=beta[:, c, h:h + 1]),
                         R=[pVt, bbeta], W=[bbv])
                chk(35)
                bankW, offW, pW = P1()
                wps = v3(bankW, offW, 128, 4, 64)
                for h in range(4):
                    k.op(pe, lambda e: e.matmul(wps[:, h, :], bk_[:, h, :], Rf[:, h, :], start=True, stop=True),
                         R=[bbk, bRf], W=[pW], sig=(h == 3))
                k.op(act, lambda e: e.copy(wT[:], wps), R=[pW], W=[bwT])
                bankU, pU = P2()
                ups = v3(bankU, 0, 64, 4, 128)
                for h in range(4):
                    k.op(pe, lambda e: e.matmul(ups[:, h, :], Rf[:, h, :], bv_[:, h, :], start=True, stop=True),
                         R=[bbv, bRf], W=[pU], sig=(h == 3))
                k.op(act, lambda e: e.copy(u_[:], ups), R=[pU], W=[bu])
                for h in range(4):
                    k.op(pool, lambda e: e.tensor_tensor(out=qd[:, h, :], in0=cs[h][0][:, tok], in1=EGb[:, h, :], op=ALU.mult),
                         R=[cs[h][1], bEGb], W=[bqd])
                chk(4)
                bankWS, pWS = P2()
                wsps = v3(bankWS, 0, 64, 4, 128)
                for h in range(4):
                    k.op(pe, lambda e: e.matmul(wsps[:, h, :], wT[:, h, :], Sg[:, h, :], start=True, stop=True),
                         R=[bwT, bSg], W=[pWS], sig=(h == 3))
                k.op(dve, lambda e: e.tensor_tensor(out=vn[:], in0=u_[:], in1=wsps, op=ALU.subtract), R=[bu, pWS], W=[bvn])
                bankO, offO, pO = P1()
                ops_ = v3(bankO, offO, 128, 4, 64)
                for h in range(4):
                    k.op(pe, lambda e: e.matmul(ops_[:, h, :], Sg[:, h, :], qd[:, h, :], start=True, stop=False),
                         R=[bSg, bqd], W=[pO], sig=False)
                    k.op(pe, lambda e: e.matmul(ops_[:, h, :], vn[:, h, :], qkd[:, h, :], start=False, stop=True),
                         R=[bvn, bqkd], W=[pO], sig=(h == 3))
                k.op(act, lambda e: e.copy(oT[:, 0:4, tok], ops_), R=[pO], W=[boT])
                bankS, pS = P2()
                dsps = v3(bankS, 0, 128, 4, 128)
                for h in range(4):
                    k.op(pe, lambda e: e.matmul(dsps[:, h, :], kd_[:, h, :], vn[:, h, :], start=True, stop=True),
                         R=[bkd, bvn], W=[pS], sig=(h == 3))
                for h in range(4):
                    k.op(dve, lambda e: e.scalar_tensor_tensor(out=Sg[:, h, :], in0=Sg[:, h, :], scalar=EGb[:, h, 63:64],
                                                               in1=dsps[:, h, :], op0=ALU.mult, op1=ALU.add),
                         R=[bSg, bEGb, pS], W=[bSg])

                chk(5)
            chk(6)
            for h in range(8):
                k.op(pool, lambda e: e.tensor_tensor(out=sq[:], in0=oT[:, h, :], in1=oT[:, h, :], op=ALU.mult), R=[boT], W=[bsq])
                bank, off, pb = P1()
                k.op(pe, lambda e: e.matmul(bank[:, off:off + GT], ones[:], sq[:], start=True, stop=True), R=[bones, bsq], W=[pb])
                k.op(act, lambda e: e.activation(out=rn[:], in_=bank[:, off:off + GT], func=AF.Sqrt, scale=1.0 / 128, bias=RMS_EPS),
                     R=[pb], W=[brn])
                k.op(dve, lambda e: e.reciprocal(out=rn[:], in_=rn[:]), R=[brn], W=[brn])
                wcol = nw[:, 0:1] if h < 4 else nw[:, 1:2]
                k.op(dve, lambda e: e.scalar_tensor_tensor(out=rn[:], in0=oT[:, h, :], scalar=wcol, in1=rn[:], op0=ALU.mult,
                                                           op1=ALU.mult), R=[boT, bnw, brn], W=[brn])
                k.op(pool, lambda e: e.tensor_tensor(out=mixT[:, h, :], in0=rn[:], in1=sz[h][0][:], op=ALU.mult),
                     R=[brn, sz[h][1]], W=[bmixT])
            if dbg is not None:
                k.dma(sp, [(dbg[:, :, g * GT:(g + 1) * GT], oT[:])], sdbg, R=[boT], W=[])
            for s in range(2):
                t = g * 2 + s
                yy, byy = xin[s]
                for hf in range(2):
                    bank2, pb2 = P2()
                    for mc in range(8):
                        k.op(pe, lambda e: e.matmul(bank2[:, :], mixT[:, mc, s * 128:(s + 1) * 128], Wo[:, mc, hf * 512:(hf + 1) * 512],
                                                    start=(mc == 0), stop=(mc == 7)), R=[bmixT, bWo], W=[pb2], sig=(mc == 7))
                    k.op(dve, lambda e: e.scalar_tensor_tensor(out=yy[:, hf * 512:(hf + 1) * 512], in0=xin[s][0][:, hf * 512:(hf + 1) * 512],
                                                               scalar=ALPHA, in1=bank2[:, :], op0=ALU.mult, op1=ALU.add),
                         R=[pb2], W=[byy])
                layer_norm(k, yy, byy, st[s][0], st[s][1], gbt, bgb)
                k.dma(sp, [(dst[t * 128:(t + 1) * 128, :], yy[:])], sy[s], R=[byy], W=[dstb[t]])
        k.barrier()


def build_mix_only(T, stop=0):
    nc = bass.Bass("TRN2", target_bir_lowering=False)
    def inp(name, shape):
        return nc.dram_tensor(name, shape, F32, kind="ExternalInput").ap()
    x1 = inp("x1", [T, D])
    w_in = inp("w_in", [D, PROJ]); conv_w = inp("conv_w", [4, 1536]); a_log = inp("a_log", [1, 4])
    dt_bias = inp("dt_bias", [1, 4]); gdn_nw = inp("gdn_norm_w", [1, 128]); w_gk = inp("w_gk", [16, 256])
    b_gk = inp("b_gk", [1, 256]); gla_nw = inp("gla_norm_w", [1, 128]); w_out = inp("w_out", [D, D])
    lg = inp("ln2_g", [1, D]); lb = inp("ln2_b", [1, D])
    out = nc.dram_tensor("out", [T, D], F32, kind="ExternalOutput").ap()
    dbg = nc.dram_tensor("dbg", [128, 8, T], F32, kind="ExternalOutput").ap()
    k = K(nc)
    srcb = [Buf("x") for _ in range(T // 128)]
    dstb = [Buf("o") for _ in range(T // 128)]
    phase_mix(k, "mx", x1, out, srcb, dstb, w_in, conv_w, a_log, dt_bias, gdn_nw, w_gk, b_gk, gla_nw, w_out, lg, lb, T, dbg=dbg, stop=stop)
    return nc


W_NAMES = [("ffn1_w_gate", [D, FF]), ("ffn1_w_up", [D, FF]), ("ffn1_w_down", [FF, D]), ("ln1_g", [1, D]), ("ln1_b", [1, D]),
           ("w_in", [D, PROJ]), ("conv_w", [4, 1536]), ("a_log", [1, 4]), ("dt_bias", [1, 4]), ("gdn_norm_w", [1, 128]),
           ("w_gk", [16, 256]), ("b_gk", [1, 256]), ("gla_norm_w", [1, 128]), ("w_out", [D, D]),
           ("ln2_g", [1, D]), ("ln2_b", [1, D]),
           ("ffn2_w_gate", [D, FF]), ("ffn2_w_up", [D, FF]), ("ffn2_w_down", [FF, D]), ("ln3_g", [1, D]), ("ln3_b", [1, D])]


def build_full(T):
    nc = bass.Bass("TRN2", target_bir_lowering=False)
    x = nc.dram_tensor("x", [T, D], F32, kind="ExternalInput").ap()
    w = {n: nc.dram_tensor(n, shp, F32, kind="ExternalInput").ap() for n, shp in W_NAMES}
    out = nc.dram_tensor("out", [T, D], F32, kind="ExternalOutput").ap()
    X1 = nc.dram_tensor("X1s", [T, D], F32, kind="Internal").ap()
    X2 = nc.dram_tensor("X2s", [T, D], F32, kind="Internal").ap()
    k = K(nc)
    nt = T // 128
    bx = [Buf("x") for _ in range(nt)]
    b1 = [Buf("x1") for _ in range(nt)]
    b2 = [Buf("x2") for _ in range(nt)]
    bo = [Buf("o") for _ in range(nt)]
    phase_ffn(k, "f1", x, X1, bx, b1, w["ffn1_w_gate"], w["ffn1_w_up"], w["ffn1_w_down"], w["ln1_g"], w["ln1_b"], T)
    phase_mix(k, "mx", X1, X2, b1, b2, w["w_in"], w["conv_w"], w["a_log"], w["dt_bias"], w["gdn_norm_w"], w["w_gk"],
              w["b_gk"], w["gla_norm_w"], w["w_out"], w["ln2_g"], w["ln2_b"], T)
    phase_ffn(k, "f2", X2, out, b2, bo, w["ffn2_w_gate"], w["ffn2_w_up"], w["ffn2_w_down"], w["ln3_g"], w["ln3_b"], T)
    return nc


def kernel(**inputs):
    x = np.ascontiguousarray(np.asarray(inputs["x"], dtype=np.float32))
    B, L, _ = x.shape
    wmaps = {}
    for n, shp in W_NAMES:
        wmaps[n] = np.ascontiguousarray(np.asarray(inputs[n], dtype=np.float32)[0].reshape(shp))
    nc = build_full(L)
    in_maps = []
    for c in range(NCORES):
        m = dict(wmaps)
        m["x"] = x[c % B]
        in_maps.append(m)
    res = run_bass_kernel_spmd(nc, in_maps, core_ids=list(range(NCORES)))
    return np.stack([np.asarray(res.results[b]["out"], dtype=np.float32) for b in range(B)], axis=0)
```
